# Optimizing a Trainium2 kernel written in Bass

```python
import math
import jax, jax.numpy as jnp
from jax import lax
import numpy as np

D_MODEL = 4096
BATCH = 1
SEQ = 8192
DEPTH = 1

HEAD_DIM = 128
N_HEADS = D_MODEL // (2 * HEAD_DIM)
N_KV_HEADS = 4
GROUP = N_HEADS // N_KV_HEADS
ATTN_WIDTH = N_HEADS * HEAD_DIM
KV_WIDTH = N_KV_HEADS * HEAD_DIM
WINDOW = 128
BLOCK = 128
POOL_WIDTH = D_MODEL - ATTN_WIDTH
POOL_WINDOWS = (2, 4, 8, 16)
N_POOL_GROUPS = len(POOL_WINDOWS)
POOL_GROUP_W = POOL_WIDTH // N_POOL_GROUPS
MIX_WIDTH = ATTN_WIDTH + POOL_WIDTH
IN_COLS = ATTN_WIDTH + 2 * KV_WIDTH + POOL_WIDTH
D_FF = int(math.ceil(8 * D_MODEL / 3 / 256)) * 256
RMS_EPS = 1e-6
NEG_INF = -1e30

kernel_name = "hybrid_swa_alibi_multiscale_pool_block"


def _rmsnorm(x, g):
    xf = x.astype(jnp.float32)
    y = xf * lax.rsqrt(jnp.mean(xf * xf, axis=-1, keepdims=True) + RMS_EPS)
    return (y * g.astype(jnp.float32)).astype(x.dtype)


def _alibi_slopes(n):
    return 2.0 ** (-8.0 * jnp.arange(1, n + 1, dtype=jnp.float32) / n)


def _banded_window_attention(q, k, v, sink_logits):
    B, S, H, D = q.shape
    n = S // BLOCK
    pad = ((0, 0), (BLOCK, BLOCK), (0, 0), (0, 0))
    kp = jnp.pad(k, pad).reshape(B, n + 2, BLOCK, N_KV_HEADS, D)
    vp = jnp.pad(v, pad).reshape(B, n + 2, BLOCK, N_KV_HEADS, D)
    kb = jnp.concatenate([kp[:, :-2], kp[:, 1:-1], kp[:, 2:]], axis=2)
    vb = jnp.concatenate([vp[:, :-2], vp[:, 1:-1], vp[:, 2:]], axis=2)
    qb = q.reshape(B, n, BLOCK, N_KV_HEADS, GROUP, D)
    s = jnp.einsum('bnqkgd,bnskd->bnkgqs', qb, kb,
                   preferred_element_type=jnp.float32)
    qi = jnp.arange(BLOCK)[:, None]
    kj = jnp.arange(3 * BLOCK)[None, :]
    dist = kj - BLOCK - qi
    kpos = jnp.arange(n)[:, None] * BLOCK - BLOCK + jnp.arange(3 * BLOCK)[None, :]
    valid = (jnp.abs(dist) <= WINDOW)[None] & ((kpos >= 0) & (kpos < S))[:, None, :]
    slopes = _alibi_slopes(N_HEADS).reshape(N_KV_HEADS, GROUP, 1, 1)
    bias = -slopes * jnp.abs(dist).astype(jnp.float32)
    logits = jnp.where(valid[None, :, None, None], s + bias[None, None], NEG_INF)
    sink = sink_logits.astype(jnp.float32).reshape(1, 1, N_KV_HEADS, GROUP, 1, 1)
    lse = jnp.logaddexp(jax.nn.logsumexp(logits, axis=-1, keepdims=True), sink)
    p = jnp.exp(logits - lse)
    o = jnp.einsum('bnkgqs,bnskd->bnqkgd', p.astype(v.dtype), vb)
    return o.reshape(B, S, H * D)


def _multiscale_pool(p, pool_w, pool_scale):
    B, S, _ = p.shape
    pf = p.astype(jnp.float32).reshape(B, S, N_POOL_GROUPS, POOL_GROUP_W)
    cs = jnp.concatenate([jnp.zeros((B, 1, N_POOL_GROUPS, POOL_GROUP_W), jnp.float32),
                          jnp.cumsum(pf, axis=1)], axis=1)
    t = jnp.arange(S)
    outs = []
    for gi, w in enumerate(POOL_WINDOWS):
        left = w // 2
        right = w - 1 - left
        lo = jnp.clip(t - left, 0, S)
        hi = jnp.clip(t + right + 1, 0, S)
        seg = cs[:, :, gi]
        win_sum = jnp.take(seg, hi, axis=1) - jnp.take(seg, lo, axis=1)
        cnt = (hi - lo).astype(jnp.float32)[None, :, None]
        outs.append(win_sum / cnt - pf[:, :, gi])
    u = jnp.stack(outs, axis=2).astype(p.dtype)
    y = jnp.einsum('bsgc,gcd->bsgd', u, pool_w).reshape(B, S, POOL_WIDTH)
    return y * pool_scale


def setup_inputs(seed: int = 0) -> dict:
    key = jax.random.key(seed)
    ks = jax.random.split(key, 13)
    f32 = jnp.float32
    nrm = lambda k, shape, fan_in: jax.random.normal(k, shape, f32) * (fan_in ** -0.5)
    return {
        "x": jax.random.normal(ks[0], (BATCH, SEQ, D_MODEL), f32),
        "norm1_g": 1.0 + 0.02 * jax.random.normal(ks[1], (D_MODEL,), f32),
        "w_in": nrm(ks[2], (D_MODEL, IN_COLS), D_MODEL),
        "q_norm_g": 1.0 + 0.02 * jax.random.normal(ks[3], (HEAD_DIM,), f32),
        "k_norm_g": 1.0 + 0.02 * jax.random.normal(ks[4], (HEAD_DIM,), f32),
        "sink_logits": 0.5 * jax.random.normal(ks[5], (N_HEADS,), f32),
        "pool_w": nrm(ks[6], (N_POOL_GROUPS, POOL_GROUP_W, POOL_GROUP_W), POOL_GROUP_W),
        "pool_scale": 1.0 + 0.02 * jax.random.normal(ks[7], (POOL_WIDTH,), f32),
        "w_out": nrm(ks[8], (MIX_WIDTH, D_MODEL), MIX_WIDTH),
        "norm2_g": 1.0 + 0.02 * jax.random.normal(ks[9], (D_MODEL,), f32),
        "w_gate": nrm(ks[10], (D_MODEL, D_FF), D_MODEL),
        "w_up": nrm(ks[11], (D_MODEL, D_FF), D_MODEL),
        "w_down": nrm(ks[12], (D_FF, D_MODEL), D_FF),
    }


def reference(x, norm1_g, w_in, q_norm_g, k_norm_g, sink_logits, pool_w, pool_scale,
              w_out, norm2_g, w_gate, w_up, w_down):
    B, S, _ = x.shape
    for _layer in range(DEPTH):
        h = _rmsnorm(x, norm1_g)
        proj = h @ w_in
        q = proj[..., :ATTN_WIDTH].reshape(B, S, N_HEADS, HEAD_DIM)
        k = proj[..., ATTN_WIDTH:ATTN_WIDTH + KV_WIDTH].reshape(B, S, N_KV_HEADS, HEAD_DIM)
        v = proj[..., ATTN_WIDTH + KV_WIDTH:ATTN_WIDTH + 2 * KV_WIDTH].reshape(B, S, N_KV_HEADS, HEAD_DIM)
        p = proj[..., ATTN_WIDTH + 2 * KV_WIDTH:]
        q = _rmsnorm(q, q_norm_g) * jnp.asarray(HEAD_DIM ** -0.5, q.dtype)
        k = _rmsnorm(k, k_norm_g)
        attn_out = _banded_window_attention(q, k, v, sink_logits)
        pool_out = _multiscale_pool(p, pool_w, pool_scale)
        mixed = jnp.concatenate([attn_out, pool_out], axis=-1)
        x = x + mixed @ w_out
        h2 = _rmsnorm(x, norm2_g)
        x = x + (jax.nn.silu(h2 @ w_gate) * (h2 @ w_up)) @ w_down
    return x
```

```python
import contextlib
import numpy as np
import concourse.bass as bass
import concourse.mybir as mybir
from concourse.alu_op_type import AluOpType as ALU
from concourse.bass_utils import run_bass_kernel_spmd

F32 = mybir.dt.float32
BF16 = mybir.dt.bfloat16
AF = mybir.ActivationFunctionType

PE, ACT, DVE, POOL, SP = "pe", "act", "dve", "pool", "sp"
ENGS = (PE, ACT, DVE, POOL, SP)
RMS_EPS = 1e-6
NEG = -1.0e30


class Cfg:
    def __init__(self, D=4096, FF=11008, H=16, KV=4, SEQ=8192, NCORES=8, T=512, NG=8):
        self.D, self.FF, self.H, self.KV, self.SEQ, self.NCORES, self.T, self.NG = D, FF, H, KV, SEQ, NCORES, T, NG
        self.HD = 128
        self.G = H // KV
        self.GW = self.G * 128
        assert self.GW <= 512
        self.KC = D // 128
        self.AW = H * 128
        self.PW = D - self.AW
        self.PC = self.PW // 128
        self.PGC = self.PC // 4
        self.PGW = self.PGC * 128
        self.INC = H + 2 * KV + self.PC
        self.FFC = FF // 128
        self.NGRP = (self.FFC + NG - 1) // NG
        self.TPC = SEQ // NCORES
        self.NPASS = self.TPC // T
        self.NT = T // 128
        self.NE = self.NT + 2
        self.TT = self.NE * 128
        self.CB = D // 512
        self.KQ = max(1, self.KC // 8)
        self.K8 = self.KC // self.KQ
        self.TRB = (self.KC + 7) // 8
        assert self.TRB <= 4 and self.NT == 4


FULL = Cfg()


class Prog:
    def __init__(self):
        self.q = {e: [] for e in ENGS}
        self.cnt = {}
        self.waited = {e: {} for e in ENGS}
        self.pending = {e: [] for e in ENGS}
        self.nbar = 0

    def emit(self, eng, fn, waits=(), sig=None, amt=1):
        ws = []
        allw = list(waits) + self.pending[eng]
        self.pending[eng] = []
        for tok in allw:
            if tok is None:
                continue
            s, v = tok
            if v <= 0 or self.waited[eng].get(s, 0) >= v:
                continue
            self.waited[eng][s] = v
            ws.append((s, v))
        tok = None
        if sig is not None:
            self.cnt[sig] = self.cnt.get(sig, 0) + amt
            tok = (sig, self.cnt[sig])
        self.q[eng].append((ws, fn, sig, amt))
        return tok

    def after(self, eng, tok):
        self.pending[eng].append(tok)


def build_program(cfg):
    c = cfg
    D, T, TT, KC, H, KV, G, GW = c.D, c.T, c.TT, c.KC, c.H, c.KV, c.G, c.GW
    nc = bass.Bass("TRN2", target_bir_lowering=False)
    P = Prog()

    def din(name, shape):
        return nc.dram_tensor(name, list(shape), F32, kind="ExternalInput").ap()

    x_ext = din("x_ext", [c.TPC + 256, D])
    g1b_d = din("g1b", [128, D])
    g2b_d = din("g2b", [128, D])
    gq_d = din("gq", [128, 1])
    gk_d = din("gk", [128, 1])
    sink_d = din("sinkb", [128, H])
    pscale_d = din("pscale", [128, c.PC])
    bias_d = din("bias_all", [128, 3 * H * 128])
    kmask_d = din("kmask", [128, c.NPASS * c.NE])
    rc_d = din("rcb", [128, c.NPASS * 4 * T])
    ident_d = din("ident", [128, 128])
    w_in_d = din("w_in_t", [c.INC, 128, KC * 128])
    w_gu_d = din("w_gu_t", [c.FFC, 2, 128, KC * 128])
    w_out_d = din("w_out_t", [c.CB * c.KQ, 128, c.K8 * 512])
    w_dn_d = din("w_dn_t", [c.NGRP * c.CB, 128, c.NG * 512])
    pool_d = din("pool_w_t", [4, 128, c.PGC * c.PGW])
    out_d = nc.dram_tensor("out", [c.TPC, D], F32, kind="ExternalOutput").ap()

    def sb(name, shape, dt=F32):
        return nc.alloc_sbuf_tensor(name, list(shape), dt)

    ident_f = sb("ident_f", [128, 128])
    ident_b = sb("ident_b", [128, 128], BF16)
    ones_b = sb("ones_b", [128, 128], BF16)
    ones_f = sb("ones_f", [128, 128])
    eps_t = sb("eps_t", [128, 1])
    gq_t = sb("gq_t", [128, 1])
    gqs_t = sb("gqs_t", [128, 1])
    gk_t = sb("gk_t", [128, 1])
    sink_t = sb("sink_t", [128, H])
    es_t = sb("es_t", [128, H])
    pscale_t = sb("pscale_t", [128, c.PC])
    kmask_t = sb("kmask_t", [128, c.NPASS * c.NE])
    ssq = sb("ssq", [128, 8])
    lnv = sb("lnv", [128, 8])
    rstd = sb("rstd", [128, 8])
    scr = sb("scr", [128, 8])

    base = (nc._sbuf_addr_for_side("left") + 63) // 64 * 64
    LIMIT = 229344

    def at(name, shape, dt, off):
        nbytes = int(np.prod(shape[1:])) * (4 if dt == F32 else 2)
        assert base + off + nbytes <= LIMIT, (name, base, off, nbytes)
        return nc.alloc_sbuf_tensor_at(name, list(shape), dt, offset=base + off)

    KiB = 1024
    X1_SZ = c.NT * D * 4
    RA_SZ = max(KC * TT * 2, KC * T * 2 + H * 128 * 4 + 2 * c.PGC * c.PGW * 2, 6 * KC * 128 * 2)
    RB_SZ = max(2 * D * 4 + D * 4, (H * T + KV * TT + c.NE * KV * 128 + c.PC * T) * 2,
                KC * T * 2 + max(D * 4, 2 * c.NG * T * 2))
    WT_SZ = max(KC * 128 * 2, c.K8 * 512 * 2, c.NG * 512 * 2)
    RC_SZ = max(3 * WT_SZ, 3 * H * 128 * 4)
    oX1, oRA = 0, X1_SZ
    oRB = oRA + RA_SZ
    oRC = oRB + RB_SZ
    oRD = oRC + RC_SZ

    x1 = at("x1", [128, c.NT, D], F32, oX1)
    hT = at("hT", [128, KC, TT], BF16, oRA)
    mixedT = at("mixedT", [128, KC, T], BF16, oRA)
    esink_all = at("esink_all", [128, H, 128], F32, oRA + KC * T * 2)
    pwt = [at(f"pwt{i}", [128, c.PGC, c.PGW], BF16, oRA + KC * T * 2 + H * 128 * 4 + i * c.PGC * c.PGW * 2)
           for i in range(2)]
    guw = [at(f"guw{i}", [128, KC, 128], BF16, oRA + i * KC * 128 * 2) for i in range(6)]
    xh = [at(f"xh{i}", [128, D], F32, oRB + i * D * 4) for i in range(2)]
    g1b = at("g1b_s", [128, D], F32, oRB + 2 * D * 4)
    qT = at("qT", [128, H, T], BF16, oRB)
    kT = at("kT", [128, KV, TT], BF16, oRB + H * T * 2)
    vtok = at("vtok", [128, c.NE, KV * 128], BF16, oRB + H * T * 2 + KV * TT * 2)
    uT = at("uT", [128, c.PC, T], BF16, oRB + H * T * 2 + KV * TT * 2 + c.NE * KV * 128 * 2)
    h2T = at("h2T", [128, KC, T], BF16, oRB)
    g2b = at("g2b_s", [128, D], F32, oRB + KC * T * 2)
    actb = [at(f"act{i}", [128, c.NG, T], BF16, oRB + KC * T * 2 + i * c.NG * T * 2) for i in range(2)]
    wring = [at(f"wr{i}", [128, WT_SZ // 2], BF16, oRC + i * WT_SZ) for i in range(3)]
    bias_all = at("bias_all_s", [128, 3, H * 128], F32, oRC)
    htok = [at(f"htok{i}", [128, D], BF16, oRD + i * D * 2) for i in range(2)]
    o = oRD
    sqb = at("sqb", [128, 512], F32, o); o += 2048
    lnb = at("lnb", [128, 512], F32, o); o += 2048
    rsb = at("rsb", [128, 512], F32, o); o += 2048
    pbuf = at("pbuf", [128, 528], F32, o); o += 2112
    sA = at("sA", [128, 528], F32, o); o += 2112
    sB = at("sB", [128, 528], F32, o); o += 2112
    vtmp = at("vtmp", [128, 512], BF16, o); o += 1024
    rcb = at("rcb_s", [128, 4, T], F32, o); o += 4 * T * 4
    o = oRD
    lg = []
    for i in range(3):
        lg.append(at(f"lg{i}", [128, 512], F32, o)); o += 2048
    PTb = []
    for i in range(2):
        PTb.append(at(f"PT{i}", [128, 3, 512], BF16, o)); o += 3072
    dnb = at("dnb", [128, 512], F32, o); o += 2048
    lnd = at("lnd", [128, 512], F32, o); o += 2048
    rden = at("rden", [128, 512], F32, o); o += 2048
    sg = [at(f"sg{i}", [128, 512], F32, oRD + i * 2048) for i in range(2)]

    ps = nc.alloc_psum_tensor("ps", [128, 8, 512], F32)

    def bank(b, n=512):
        return ps[:, b, 0:n]

    def bank_bf(b):
        return ps[:, b, :].bitcast(BF16).rearrange("p (a b) -> p a b", a=8)

    def mm(out, lhsT, rhs, start, stop):
        return lambda e: e.matmul(out=out, lhsT=lhsT, rhs=rhs, start=start, stop=stop)

    def tr(out, in_):
        return lambda e: e.transpose(out=out, in_=in_, identity=ident_b[:, :])

    def act(out, in_, func, bias=None, scale=None, accum_out=None):
        kw = {}
        if bias is not None:
            kw["bias"] = bias
        if scale is not None:
            kw["scale"] = scale
        if accum_out is not None:
            kw["accum_out"] = accum_out
        return lambda e: e.activation(out=out, in_=in_, func=func, **kw)

    def tt(out, in0, in1, op):
        return lambda e: e.tensor_tensor(out=out, in0=in0, in1=in1, op=op)

    def stt(out, in0, scalar, in1, op0, op1):
        return lambda e: e.scalar_tensor_tensor(out=out, in0=in0, scalar=scalar, in1=in1, op0=op0, op1=op1)

    def ts(out, in0, s1, op0):
        return lambda e: e.tensor_scalar(out=out, in0=in0, scalar1=s1, scalar2=None, op0=op0)

    def cp(out, in_):
        return lambda e: e.tensor_copy(out=out, in_=in_)

    def dma(out, in_):
        return lambda e: e.dma_start(out=out, in_=in_)

    def mset(ap, v):
        return lambda e: e.memset(ap, v)

    def copy_on(eng, out, in_):
        if eng == ACT:
            return act(out, in_, AF.Copy)
        return cp(out, in_)

    def barrier():
        P.nbar += 1
        toks = [
            P.emit(ACT, act(scr[:, 0:1], eps_t[:, 0:1], AF.Copy), sig="bar_act"),
            P.emit(DVE, mset(scr[:, 1:2], 0.0), sig="bar_dve"),
            P.emit(POOL, mset(scr[:, 2:3], 0.0), sig="bar_pool"),
        ]
        for e in ENGS:
            for t_ in toks:
                P.after(e, t_)

    cst = []
    for dst, src in ((ident_f, ident_d), (gq_t, gq_d), (gk_t, gk_d), (sink_t, sink_d),
                     (pscale_t, pscale_d), (kmask_t, kmask_d)):
        cst.append(P.emit(SP, dma(dst[:, :], src), sig="cst", amt=16))
    t_cst = cst[-1]
    P.emit(DVE, cp(ident_b[:, :], ident_f[:, :]), waits=[t_cst])
    P.emit(DVE, mset(ones_b[:, :], 1.0))
    P.emit(DVE, mset(ones_f[:, :], 1.0))
    P.emit(DVE, mset(eps_t[:, :], RMS_EPS))
    P.emit(DVE, ts(gqs_t[:, :], gq_t[:, :], float(c.HD) ** -0.5, ALU.mult))
    P.emit(ACT, act(es_t[:, :], sink_t[:, :], AF.Exp), waits=[t_cst])
    barrier()

    ring_use = {}

    def ring_full_tok(name, slot):
        return (f"{name}{slot}", 16 * ring_use[(name, slot)])

    ost_total = [None]

    def ring_load(name, slot, dst_ap, src_ap, free_tok):
        ring_use[(name, slot)] = ring_use.get((name, slot), 0) + 1
        return P.emit(POOL, dma(dst_ap, src_ap), waits=[free_tok], sig=f"{name}{slot}", amt=16)

    for p in range(c.NPASS):
        t_xl = []
        for e in range(c.NE):
            r0 = p * T + e * 128
            dst = xh[0][:, :] if e == 0 else (xh[1][:, :] if e == c.NE - 1 else x1[:, e - 1, :])
            t_xl.append(P.emit(SP, dma(dst, x_ext[r0:r0 + 128, :]), waits=[ost_total[0]], sig=f"xl{e}", amt=16))
        t_g1 = P.emit(SP, dma(g1b[:, :], g1b_d), sig="g1", amt=16)
        t_rc = None
        t_trl = {}
        t_ev = {}
        evi = 0
        for e in range(c.NE):
            xs = xh[0][:, :] if e == 0 else (xh[1][:, :] if e == c.NE - 1 else x1[:, e - 1, :])
            hb = htok[e % 2]
            t_sq = P.emit(ACT, act(hb[:, :], xs, AF.Square, accum_out=ssq[:, e:e + 1]),
                          waits=[t_xl[e], t_trl.get(e - 2)], sig="a_x")
            t_ln = P.emit(ACT, act(lnv[:, e:e + 1], ssq[:, e:e + 1], AF.Ln, bias=eps_t[:, 0:1], scale=1.0 / D),
                          waits=[t_sq], sig="a_x")
            t_rs = P.emit(ACT, act(rstd[:, e:e + 1], lnv[:, e:e + 1], AF.Exp, scale=-0.5), waits=[t_ln], sig="a_x")
            t_h = P.emit(DVE, stt(hb[:, :], xs, rstd[:, e:e + 1], g1b[:, :], ALU.mult, ALU.mult),
                         waits=[t_rs, t_g1, t_xl[e]], sig="d_h")
            for b in range(c.TRB):
                bk = (e % 2) * 4 + b
                nk = min(8, KC - b * 8)
                tk = None
                for j in range(nk):
                    kc = b * 8 + j
                    tk = P.emit(PE, tr(bank_bf(bk)[:, j, :], hb[:, kc * 128:(kc + 1) * 128]),
                                waits=[t_h, t_ev.get(bk)], sig=("pe_tr" if j == nk - 1 else None))
                eng = ACT if evi % 2 == 0 else DVE
                evi += 1
                t_ev[bk] = P.emit(eng, copy_on(eng, hT[:, b * 8:b * 8 + nk, e * 128:(e + 1) * 128],
                                               bank_bf(bk)[:, 0:nk, :]),
                                  waits=[tk], sig=f"ev_{eng}")
                t_trl[e] = tk
        barrier()

        t_rc = P.emit(SP, dma(rcb[:, :, :], rc_d[:, p * 4 * T:(p + 1) * 4 * T].rearrange("q (a b) -> q a b", a=4)),
                      sig="rc", amt=16)
        slot_free = {}
        acc_free = {}
        sec_free = {}
        sq_free = [None]
        rs_free = [None]
        vt_free = [None]
        pb_free = [None]
        deferred = []
        dve_prev = [None]

        def dve_chain(fn, waits=()):
            tok_ = P.emit(DVE, fn, waits=list(waits) + [dve_prev[0]], sig="d_c")
            dve_prev[0] = tok_
            return tok_

        sub = 0
        for m in range(c.INC):
            slot = m % 3
            wt = wring[slot][:, 0:KC * 128].rearrange("q (k j) -> q k j", k=KC)
            ring_load("wf", slot, wring[slot][:, 0:KC * 128], w_in_d[m], slot_free.get(slot))
            t_full = ring_full_tok("wf", slot)
            if m < H:
                kind, subs = "q", [(128, T)]
            elif m < H + KV:
                kind, subs = "k", [(0, 512), (512, TT - 512)]
            elif m < H + 2 * KV:
                kind, subs = "v", [(0, 512), (512, TT - 512)]
            else:
                kind, subs = "p", [(120, 512)]
            for si, (t0, n) in enumerate(subs):
                ab = sub % 2
                sbk = 2 + sub % 2
                sub += 1
                last_sub = si == len(subs) - 1
                tk = None
                for kc in range(KC):
                    tk = P.emit(PE, mm(bank(ab, n), wt[:, kc, :], hT[:, kc, t0:t0 + n], kc == 0, kc == KC - 1),
                                waits=[t_full, acc_free.get(ab)], sig=("pe_b" if kc == KC - 1 else None))
                t_acc = tk
                if kind == "p":
                    for kc in range(KC):
                        tk = P.emit(PE, mm(bank(sbk, 16), wt[:, kc, :], hT[:, kc, 632:648], kc == 0, kc == KC - 1),
                                    waits=[sec_free.get(sbk)], sig=("pe_b" if kc == KC - 1 else None))
                    t_acc2 = tk
                if last_sub:
                    slot_free[slot] = tk
                for fn_ in deferred:
                    fn_()
                deferred = []
                if kind in ("q", "k"):
                    gain = gqs_t if kind == "q" else gk_t
                    if kind == "q":
                        dst = qT[:, m, 0:n]
                    else:
                        dst = kT[:, m - H, t0:t0 + n]
                    t_sq = P.emit(ACT, act(sqb[:, 0:n], bank(ab, n), AF.Square), waits=[t_acc, sq_free[0]], sig="a_b")

                    def later(ab=ab, sbk=sbk, n=n, t_sq=t_sq, gain=gain, dst=dst):
                        t_ss = P.emit(PE, mm(bank(sbk, n), ones_f[:, :], sqb[:, 0:n], True, True),
                                      waits=[t_sq, sec_free.get(sbk)], sig="pe_b2")
                        sq_free[0] = t_ss
                        t_l = P.emit(ACT, act(lnb[:, 0:n], bank(sbk, n), AF.Ln, bias=eps_t[:, 0:1], scale=1.0 / c.HD),
                                     waits=[t_ss], sig="a_b")
                        sec_free[sbk] = t_l
                        t_r = P.emit(ACT, act(rsb[:, 0:n], lnb[:, 0:n], AF.Exp, scale=-0.5),
                                     waits=[t_l, rs_free[0]], sig="a_b")
                        t_q = P.emit(DVE, stt(dst, bank(ab, n), gain[:, 0:1], rsb[:, 0:n], ALU.mult, ALU.mult),
                                     waits=[t_r], sig="d_b")
                        rs_free[0] = t_q
                        acc_free[ab] = t_q
                    deferred.append(later)
                elif kind == "v":
                    kvh = m - H - KV
                    t_vt = P.emit(ACT, act(vtmp[:, 0:n], bank(ab, n), AF.Copy), waits=[t_acc, vt_free[0]], sig="a_b")
                    acc_free[ab] = t_vt

                    def later(sbk=sbk, n=n, t0=t0, t_vt=t_vt, kvh=kvh):
                        nt_ = n // 128
                        tk_ = None
                        for j in range(nt_):
                            tk_ = P.emit(PE, tr(bank_bf(sbk)[:, j, :], vtmp[:, j * 128:(j + 1) * 128]),
                                         waits=[t_vt, sec_free.get(sbk)], sig=("pe_b2" if j == nt_ - 1 else None))
                        vt_free[0] = tk_
                        e0 = t0 // 128
                        t_c = P.emit(DVE, cp(vtok[:, e0:e0 + nt_, kvh * 128:(kvh + 1) * 128], bank_bf(sbk)[:, 0:nt_, :]),
                                     waits=[tk_], sig="d_b")
                        sec_free[sbk] = t_c
                    deferred.append(later)
                else:
                    pc = m - H - 2 * KV
                    g = pc // c.PGC
                    t_p1 = P.emit(ACT, act(pbuf[:, 0:512], bank(ab, 512), AF.Copy), waits=[t_acc, pb_free[0]], sig="a_b")
                    acc_free[ab] = t_p1
                    t_p2 = P.emit(ACT, act(pbuf[:, 512:528], bank(sbk, 16), AF.Copy), waits=[t_acc2], sig="a_b")
                    sec_free[sbk] = t_p2
                    cur, L = pbuf, 528
                    bufs = [sA, sB]
                    bi = 0
                    first = True
                    shift = 1
                    for _ in range(g + 1):
                        nxt = bufs[bi]
                        bi ^= 1
                        L2 = L - shift
                        dve_chain(tt(nxt[:, 0:L2], cur[:, 0:L2], cur[:, shift:shift + L2], ALU.add),
                                  waits=([t_p1, t_p2] if first else []))
                        first = False
                        cur, L = nxt, L2
                        shift *= 2
                    w_ = 2 << g
                    off = 8 - w_ // 2
                    nxt = bufs[bi]
                    dve_chain(tt(nxt[:, 0:T], cur[:, off:off + T], rcb[:, g, :], ALU.mult), waits=[t_rc])
                    t_u = dve_chain(tt(uT[:, pc, :], nxt[:, 0:T], pbuf[:, 8:8 + T], ALU.subtract))
                    pb_free[0] = t_u
        for fn_ in deferred:
            fn_()
        deferred = []
        barrier()

        t_bias = P.emit(SP, dma(bias_all[:, :, :], bias_d.rearrange("q (a b) -> q a b", a=3)), sig="bias", amt=16)
        t_es = P.emit(DVE, cp(esink_all[:, :, :], es_t[:, :].unsqueeze(2).to_broadcast([128, H, 128])), sig="d_c2")
        pw_tok = []
        for g in range(4):
            if g < 2:
                pw_tok.append(P.emit(POOL, dma(pwt[g][:, :, :], pool_d[g].rearrange("q (a b) -> q a b", a=c.PGC)),
                                     sig=f"pw{g}", amt=16))
        s_free = {}
        lg_free = {}
        pt_free = {}
        o_free = {}
        d_free = {}
        dn_free = [None]
        rd_free = [None]
        items = [(t, kvh) for t in range(c.NT) for kvh in range(KV)]
        si_ = 0
        pend_pv = None
        for idx, (t, kvh) in enumerate(items):
            pp = idx % 2
            t_pts = []
            for kb in range(3):
                sbk = si_ % 3
                li = si_ % 3
                si_ += 1
                e = t + kb
                t_s = P.emit(PE, mm(bank(sbk, GW).rearrange("q (a b) -> q a b", a=G), kT[:, kvh, e * 128:(e + 1) * 128],
                                    qT[:, kvh * G:(kvh + 1) * G, t * 128:(t + 1) * 128], True, True),
                             waits=[s_free.get(sbk)], sig="pe_s")
                t_lg = P.emit(DVE, tt(lg[li][:, 0:GW], bank(sbk, GW), bias_all[:, kb, kvh * GW:(kvh + 1) * GW], ALU.add),
                              waits=[t_s, t_bias, lg_free.get(li)], sig="d_lg")
                s_free[sbk] = t_lg
                t_pt = P.emit(ACT, act(PTb[pp][:, kb, 0:GW], lg[li][:, 0:GW], AF.Exp,
                                       bias=kmask_t[:, p * c.NE + e:p * c.NE + e + 1]),
                              waits=[t_lg, pt_free.get(pp)], sig="a_pt")
                lg_free[li] = t_pt
                t_pts.append(t_pt)

            def pv(idx=idx, t=t, kvh=kvh, pp=pp, t_pts=t_pts):
                ob = 4 + idx % 2
                db = 6 + idx % 2
                tk_ = None
                for kb in range(3):
                    e = t + kb
                    tk_ = P.emit(PE, mm(bank(ob, GW), vtok[:, e, kvh * 128:(kvh + 1) * 128], PTb[pp][:, kb, 0:GW],
                                        kb == 0, kb == 2),
                                 waits=[t_pts[kb], o_free.get(ob)])
                for kb in range(3):
                    tk_ = P.emit(PE, mm(bank(db, GW), ones_b[:, :], PTb[pp][:, kb, 0:GW], kb == 0, kb == 2),
                                 waits=[d_free.get(db)], sig=("pe_o" if kb == 2 else None))
                pt_free[pp] = tk_
                t_dn = P.emit(DVE, tt(dnb[:, 0:GW].rearrange("q (a b) -> q a b", a=G),
                                 bank(db, GW).rearrange("q (a b) -> q a b", a=G),
                                 esink_all[:, kvh * G:(kvh + 1) * G, :], ALU.add),
                              waits=[tk_, t_es, dn_free[0]], sig="d_dn")
                d_free[db] = t_dn
                t_l = P.emit(ACT, act(lnd[:, 0:GW], dnb[:, 0:GW], AF.Ln), waits=[t_dn], sig="a_dn")
                dn_free[0] = t_l
                t_r = P.emit(ACT, act(rden[:, 0:GW], lnd[:, 0:GW], AF.Exp, scale=-1.0), waits=[t_l, rd_free[0]], sig="a_dn")
                t_mx = P.emit(DVE, tt(mixedT[:, kvh * G:(kvh + 1) * G, t * 128:(t + 1) * 128],
                                      bank(ob, GW).rearrange("q (a b) -> q a b", a=G),
                                      rden[:, 0:GW].rearrange("q (a b) -> q a b", a=G), ALU.mult),
                              waits=[t_r], sig="d_mx")
                rd_free[0] = t_mx
                o_free[ob] = t_mx

            if pend_pv is not None:
                pend_pv()
            pend_pv = pv
        pend_pv()
        pw_free = {}
        pm_free = {}
        pmi = 0
        for g in range(4):
            if g >= 2:
                pw_tok.append(P.emit(POOL, dma(pwt[g % 2][:, :, :], pool_d[g].rearrange("q (a b) -> q a b", a=c.PGC)),
                                     waits=[pw_free.get(g % 2)], sig=f"pw{g % 2}", amt=16))
            t_pw = (f"pw{g % 2}", 16 * (p * 2 + g // 2 + 1))
            for oc in range(c.PGC):
                bk = pmi % 3
                pmi += 1
                tk = None
                for kc in range(c.PGC):
                    tk = P.emit(PE, mm(bank(bk, T), pwt[g % 2][:, kc, oc * 128:(oc + 1) * 128], uT[:, g * c.PGC + kc, :],
                                       kc == 0, kc == c.PGC - 1),
                                waits=[t_pw, pm_free.get(bk), s_free.get(bk)],
                                sig=("pe_pm" if kc == c.PGC - 1 else None))
                ch = g * c.PGC + oc
                pm_free[bk] = P.emit(ACT, act(mixedT[:, H + ch, :], bank(bk, T), AF.Copy, scale=pscale_t[:, ch:ch + 1]),
                                     waits=[tk], sig="a_pm")
            pw_free[g % 2] = tk
        barrier()

        t_g2 = P.emit(SP, dma(g2b[:, :], g2b_d), sig="g2", amt=16)
        slot_free = {}
        bank_free = {}
        t_evl = {}
        wi = 0
        for cb in range(c.CB):
            toks_t = [None] * c.NT
            for kq in range(c.KQ):
                slot = wi % 3
                wi += 1
                n_el = c.K8 * 512
                ring_load("wf", slot, wring[slot][:, 0:n_el], w_out_d[cb * c.KQ + kq], slot_free.get(slot))
                t_full = ring_full_tok("wf", slot)
                wt = wring[slot][:, 0:n_el].rearrange("q (k j) -> q k j", k=c.K8)
                tk = None
                for k8 in range(c.K8):
                    kc = kq * c.K8 + k8
                    for t in range(c.NT):
                        bk = (cb % 2) * 4 + t
                        lastk = kc == KC - 1
                        tk = P.emit(PE, mm(bank(bk, 512), mixedT[:, kc, t * 128:(t + 1) * 128], wt[:, k8, :],
                                           kc == 0, lastk),
                                    waits=[t_full, bank_free.get(bk)],
                                    sig=("pe_d" if (lastk or (k8 == c.K8 - 1 and t == c.NT - 1)) else None))
                        if lastk:
                            toks_t[t] = tk
                slot_free[slot] = tk
            for t in range(c.NT):
                bk = (cb % 2) * 4 + t
                xs = x1[:, t, cb * 512:(cb + 1) * 512]
                t_evl[t] = P.emit(DVE, tt(xs, bank(bk, 512), xs, ALU.add), waits=[toks_t[t]], sig="d_ev")
                bank_free[bk] = t_evl[t]
        t_trl = {}
        t_ev = {}
        evi = 0
        for t in range(c.NT):
            hb = htok[t % 2]
            t_sq = P.emit(ACT, act(hb[:, :], x1[:, t, :], AF.Square, accum_out=ssq[:, t:t + 1]),
                          waits=[t_evl[t], t_trl.get(t - 2)], sig="a_x")
            t_ln = P.emit(ACT, act(lnv[:, t:t + 1], ssq[:, t:t + 1], AF.Ln, bias=eps_t[:, 0:1], scale=1.0 / D),
                          waits=[t_sq], sig="a_x")
            t_rs = P.emit(ACT, act(rstd[:, t:t + 1], lnv[:, t:t + 1], AF.Exp, scale=-0.5), waits=[t_ln], sig="a_x")
            t_h = P.emit(DVE, stt(hb[:, :], x1[:, t, :], rstd[:, t:t + 1], g2b[:, :], ALU.mult, ALU.mult),
                         waits=[t_rs, t_g2, t_evl[c.NT - 1]], sig="d_h")
            for b in range(c.TRB):
                bk = (t % 2) * 4 + b
                nk = min(8, KC - b * 8)
                tk = None
                for j in range(nk):
                    kc = b * 8 + j
                    tk = P.emit(PE, tr(bank_bf(bk)[:, j, :], hb[:, kc * 128:(kc + 1) * 128]),
                                waits=[t_h, t_ev.get(bk), bank_free.get(bk)], sig=("pe_tr" if j == nk - 1 else None))
                eng = ACT if evi % 2 == 0 else DVE
                evi += 1
                t_ev[bk] = P.emit(eng, copy_on(eng, h2T[:, b * 8:b * 8 + nk, t * 128:(t + 1) * 128],
                                               bank_bf(bk)[:, 0:nk, :]),
                                  waits=[tk], sig=f"ev_{eng}")
                t_trl[t] = tk
        barrier()

        gu_free = {}
        dn_slot_free = {}
        gb_free = {}
        ub_free = {}
        sg_free = {}
        act_free = {}
        dbank_free = {}
        t_act_last = {}
        t_dlast = {}
        dwi = [0]

        def down_cb(grp, cb):
            n_g = min(c.NG, c.FFC - grp * c.NG)
            ab_ = actb[grp % 2]
            slot = dwi[0] % 3
            dwi[0] += 1
            ring_load("wf", slot, wring[slot][:, 0:n_g * 512], w_dn_d[grp * c.CB + cb][:, 0:n_g * 512],
                      dn_slot_free.get(slot))
            t_full = ring_full_tok("wf", slot)
            wt = wring[slot][:, 0:n_g * 512].rearrange("q (k j) -> q k j", k=n_g)
            tk_ = None
            for t in range(c.NT):
                bk = 4 + t
                for j in range(n_g):
                    tk_ = P.emit(PE, mm(bank(bk, 512), ab_[:, j, t * 128:(t + 1) * 128], wt[:, j, :],
                                        j == 0, j == n_g - 1),
                                 waits=[t_full, t_act_last[grp], dbank_free.get(bk)],
                                 sig=("pe_dn" if j == n_g - 1 else None))
                xs = x1[:, t, cb * 512:(cb + 1) * 512]
                dbank_free[bk] = P.emit(DVE, tt(xs, bank(bk, 512), xs, ALU.add), waits=[tk_], sig="d_ev")
            dn_slot_free[slot] = tk_
            t_dlast[grp] = tk_

        pend_down = []
        for ch in range(c.FFC):
            grp, j = ch // c.NG, ch % c.NG
            toks = []
            for which in range(2):
                i = 2 * ch + which
                slot = i % 6
                ring_load("gu", slot, guw[slot][:, :, :], w_gu_d[ch, which].rearrange("q (k j) -> q k j", k=KC),
                          gu_free.get(slot))
                t_full = ring_full_tok("gu", slot)
                bk = (ch % 2) * 2 + which
                bfree = gb_free.get(bk) if which == 0 else ub_free.get(bk)
                tk = None
                for kc in range(KC):
                    tk = P.emit(PE, mm(bank(bk, T), guw[slot][:, kc, :], h2T[:, kc, :], kc == 0, kc == KC - 1),
                                waits=[t_full, bfree], sig=("pe_g" if kc == KC - 1 else None))
                gu_free[slot] = tk
                toks.append(tk)
            gbk, ubk = (ch % 2) * 2, (ch % 2) * 2 + 1
            t_sg = P.emit(ACT, act(sg[ch % 2][:, 0:T], bank(gbk, T), AF.Silu), waits=[toks[0], sg_free.get(ch % 2)],
                          sig="a_sg")
            gb_free[gbk] = t_sg
            t_a = P.emit(DVE, tt(actb[grp % 2][:, j, :], sg[ch % 2][:, 0:T], bank(ubk, T), ALU.mult),
                         waits=[t_sg, toks[1], t_dlast.get(grp - 2)], sig="d_act")
            sg_free[ch % 2] = t_a
            ub_free[ubk] = t_a
            t_act_last[grp] = t_a
            last_in_grp = (j == c.NG - 1) or (ch == c.FFC - 1)
            if last_in_grp:
                for (g_, cb_) in pend_down:
                    down_cb(g_, cb_)
                pend_down = [(grp, cb_) for cb_ in range(c.CB)]
            elif pend_down:
                g_, cb_ = pend_down.pop(0)
                down_cb(g_, cb_)
        for (g_, cb_) in pend_down:
            down_cb(g_, cb_)
        t_fin = dbank_free[4 + c.NT - 1]
        for t in range(c.NT):
            r0 = p * T + t * 128
            ost_total[0] = P.emit(SP, dma(out_d[r0:r0 + 128, :], x1[:, t, :]), waits=[t_fin], sig="ost", amt=16)
        barrier()
    P.emit(SP, lambda e: e.wait_ge(sem_handles["ost"], ost_total[0][1]))

    sem_names = sorted(P.cnt.keys())
    sem_handles = {}
    with contextlib.ExitStack() as stack:
        stack.enter_context(nc.allow_low_precision("bf16 matmul operands by design"))
        for s in sem_names:
            sem_handles[s] = stack.enter_context(nc.semaphore(s))
        block = stack.enter_context(nc.Block())

        def run(eng_name):
            def body(e):
                for ws, fn, sig, amt in P.q[eng_name]:
                    for s, v in ws:
                        e.wait_ge(sem_handles[s], v)
                    inst = fn(e)
                    if sig is not None:
                        inst.then_inc(sem_handles[sig], amt)
            return body

        block.tensor(run(PE))
        block.scalar(run(ACT))
        block.vector(run(DVE))
        block.gpsimd(run(POOL))
        block.sync(run(SP))
    return nc


def host_constants(cfg):
    c = cfg
    H = c.H
    slopes = (2.0 ** (-8.0 * np.arange(1, H + 1, dtype=np.float64) / H)).astype(np.float32)
    j = np.arange(128)[:, None]
    i = np.arange(128)[None, :]
    nd = np.zeros((3, 128, 128), np.float64)
    valid = np.zeros((3, 128, 128), bool)
    for pos in range(3):
        dist = (pos - 1) * 128 + j - i
        nd[pos] = np.abs(dist)
        valid[pos] = np.abs(dist) <= 128
    bias = np.empty((128, 3, H, 128), np.float32)
    for h in range(H):
        b = -(slopes[h].astype(np.float32) * nd.astype(np.float32))
        b = np.where(valid, b, np.float32(NEG)).astype(np.float32)
        bias[:, :, h, :] = b.transpose(1, 0, 2)
    bias = np.ascontiguousarray(bias.reshape(128, 3 * H * 128))
    ident = np.eye(128, dtype=np.float32)
    return bias, ident


def per_core_constants(cfg, core):
    c = cfg
    kmask = np.zeros((128, c.NPASS * c.NE), np.float32)
    for p in range(c.NPASS):
        for e in range(c.NE):
            g0 = core * c.TPC + p * c.T + (e - 1) * 128
            if g0 < 0 or g0 >= c.SEQ:
                kmask[:, p * c.NE + e] = NEG
    rc = np.zeros((c.NPASS, 4, c.T), np.float32)
    for p in range(c.NPASS):
        s = core * c.TPC + p * c.T + np.arange(c.T)
        for gi, w in enumerate((2, 4, 8, 16)):
            left = w // 2
            right = w - 1 - left
            lo = np.clip(s - left, 0, c.SEQ)
            hi = np.clip(s + right + 1, 0, c.SEQ)
            rc[p, gi] = (1.0 / (hi - lo).astype(np.float64)).astype(np.float32)
    rcb = np.ascontiguousarray(np.broadcast_to(rc.reshape(1, -1), (128, c.NPASS * 4 * c.T)))
    return kmask, rcb


def tile_weights(cfg, w_in, w_out, w_gate, w_up, w_down, pool_w):
    c = cfg
    KC = c.KC

    def colchunks(w, nchunk):
        K = w.shape[0]
        kc = K // 128
        return np.ascontiguousarray(w.reshape(kc, 128, nchunk, 128).transpose(2, 1, 0, 3).reshape(nchunk, 128, kc * 128))

    w_in_t = colchunks(w_in, c.INC)
    wg = colchunks(w_gate, c.FFC)
    wu = colchunks(w_up, c.FFC)
    w_gu_t = np.ascontiguousarray(np.stack([wg, wu], axis=1))
    del wg, wu
    w_out_t = np.ascontiguousarray(
        w_out.reshape(c.KQ, c.K8, 128, c.CB, 512).transpose(3, 0, 2, 1, 4).reshape(c.CB * c.KQ, 128, c.K8 * 512))
    wd = np.zeros((c.NGRP * c.NG * 128, c.D), np.float32)
    wd[:c.FF] = w_down
    w_dn_t = np.ascontiguousarray(
        wd.reshape(c.NGRP, c.NG, 128, c.CB, 512).transpose(0, 3, 2, 1, 4).reshape(c.NGRP * c.CB, 128, c.NG * 512))
    del wd
    pool_w_t = np.ascontiguousarray(
        pool_w.reshape(4, c.PGC, 128, c.PGW).transpose(0, 2, 1, 3).reshape(4, 128, c.PGC * c.PGW))
    return w_in_t, w_gu_t, w_out_t, w_dn_t, pool_w_t


def make_in_maps(cfg, x, norm1_g, w_in, q_norm_g, k_norm_g, sink_logits, pool_w, pool_scale,
                 w_out, norm2_g, w_gate, w_up, w_down):
    c = cfg
    f = lambda a: np.ascontiguousarray(np.asarray(a, dtype=np.float32))
    x = f(x).reshape(c.SEQ, c.D)
    xp = np.zeros((c.SEQ + 256, c.D), np.float32)
    xp[128:128 + c.SEQ] = x
    bias, ident = host_constants(c)
    w_in_t, w_gu_t, w_out_t, w_dn_t, pool_w_t = tile_weights(
        c, f(w_in), f(w_out), f(w_gate), f(w_up), f(w_down), f(pool_w))
    shared = {
        "g1b": np.ascontiguousarray(np.broadcast_to(f(norm1_g)[None, :], (128, c.D))),
        "g2b": np.ascontiguousarray(np.broadcast_to(f(norm2_g)[None, :], (128, c.D))),
        "gq": f(q_norm_g).reshape(128, 1),
        "gk": f(k_norm_g).reshape(128, 1),
        "sinkb": np.ascontiguousarray(np.broadcast_to(f(sink_logits)[None, :], (128, c.H))),
        "pscale": np.ascontiguousarray(f(pool_scale).reshape(c.PC, 128).T),
        "bias_all": bias,
        "ident": ident,
        "w_in_t": w_in_t, "w_gu_t": w_gu_t, "w_out_t": w_out_t, "w_dn_t": w_dn_t, "pool_w_t": pool_w_t,
    }
    in_maps = []
    for core in range(c.NCORES):
        kmask, rcb = per_core_constants(c, core)
        m = dict(shared)
        m["x_ext"] = np.ascontiguousarray(xp[core * c.TPC: core * c.TPC + c.TPC + 256])
        m["kmask"] = kmask
        m["rcb"] = rcb
        in_maps.append(m)
    return in_maps


_NC_CACHE = {}


def run_cfg(cfg, inputs, trace=False):
    key = (cfg.D, cfg.FF, cfg.H, cfg.KV)
    if key not in _NC_CACHE:
        _NC_CACHE[key] = build_program(cfg)
    nc = _NC_CACHE[key]
    in_maps = make_in_maps(cfg, **inputs)
    res = run_bass_kernel_spmd(nc, in_maps, core_ids=list(range(cfg.NCORES)), trace=trace)
    out = np.concatenate([r["out"] for r in res.results], axis=0)
    return out.reshape(1, cfg.SEQ, cfg.D).astype(np.float32), res


def kernel(x, norm1_g, w_in, q_norm_g, k_norm_g, sink_logits, pool_w, pool_scale,
           w_out, norm2_g, w_gate, w_up, w_down):
    inputs = dict(x=x, norm1_g=norm1_g, w_in=w_in, q_norm_g=q_norm_g, k_norm_g=k_norm_g,
                  sink_logits=sink_logits, pool_w=pool_w, pool_scale=pool_scale, w_out=w_out,
                  norm2_g=norm2_g, w_gate=w_gate, w_up=w_up, w_down=w_down)
    out, _ = run_cfg(FULL, inputs)
    return out
```

```python
import contextlib
import numpy as np
import concourse.bass as bass
import concourse.mybir as mybir
from concourse.alu_op_type import AluOpType as ALU
from concourse.bass_utils import run_bass_kernel_spmd

F32 = mybir.dt.float32
BF16 = mybir.dt.bfloat16
AF = mybir.ActivationFunctionType

PE, ACT, DVE, POOL, SP = "pe", "act", "dve", "pool", "sp"
ENGS = (PE, ACT, DVE, POOL, SP)
RMS_EPS = 1e-6
NEG = -1.0e30


class Cfg:
    def __init__(self, D=4096, FF=11008, H=16, KV=4, SEQ=8192, NCORES=8, T=512, NG=8):
        self.D, self.FF, self.H, self.KV, self.SEQ, self.NCORES, self.T, self.NG = D, FF, H, KV, SEQ, NCORES, T, NG
        self.HD = 128
        self.G = H // KV
        self.GW = self.G * 128
        assert self.GW <= 512
        self.KC = D // 128
        self.AW = H * 128
        self.PW = D - self.AW
        self.PC = self.PW // 128
        self.PGC = self.PC // 4
        self.PGW = self.PGC * 128
        self.INC = H + 2 * KV + self.PC
        self.FFC = FF // 128
        self.NGRP = (self.FFC + NG - 1) // NG
        self.TPC = SEQ // NCORES
        self.NPASS = self.TPC // T
        self.NT = T // 128
        self.NE = self.NT + 2
        self.TT = self.NE * 128
        self.CB = D // 512
        self.KQ = max(1, self.KC // 8)
        self.K8 = self.KC // self.KQ
        self.TRB = (self.KC + 7) // 8
        assert self.TRB <= 4 and self.NT == 4


FULL = Cfg()


class Prog:
    def __init__(self):
        self.q = {e: [] for e in ENGS}
        self.cnt = {}
        self.waited = {e: {} for e in ENGS}
        self.pending = {e: [] for e in ENGS}
        self.nbar = 0

    def emit(self, eng, fn, waits=(), sig=None, amt=1):
        ws = []
        allw = list(waits) + self.pending[eng]
        self.pending[eng] = []
        for tok in allw:
            if tok is None:
                continue
            s, v = tok
            if v <= 0 or self.waited[eng].get(s, 0) >= v:
                continue
            self.waited[eng][s] = v
            ws.append((s, v))
        tok = None
        if sig is not None:
            self.cnt[sig] = self.cnt.get(sig, 0) + amt
            tok = (sig, self.cnt[sig])
        self.q[eng].append((ws, fn, sig, amt))
        return tok

    def after(self, eng, tok):
        self.pending[eng].append(tok)


def build_program(cfg):
    c = cfg
    D, T, TT, KC, H, KV, G, GW = c.D, c.T, c.TT, c.KC, c.H, c.KV, c.G, c.GW
    nc = bass.Bass("TRN2", target_bir_lowering=False)
    P = Prog()

    def din(name, shape):
        return nc.dram_tensor(name, list(shape), F32, kind="ExternalInput").ap()

    x_ext = din("x_ext", [c.TPC + 256, D])
    g1b_d = din("g1b", [128, D])
    g2b_d = din("g2b", [128, D])
    gq_d = din("gq", [128, 1])
    gk_d = din("gk", [128, 1])
    sink_d = din("sinkb", [128, H])
    pscale_d = din("pscale", [128, c.PC])
    bias_d = din("bias_all", [128, 3 * H * 128])
    kmask_d = din("kmask", [128, c.NPASS * c.NE])
    rc_d = din("rcb", [128, c.NPASS * 4 * T])
    ident_d = din("ident", [128, 128])
    w_in_d = din("w_in_t", [c.INC, 128, KC * 128])
    w_gu_d = din("w_gu_t", [c.FFC, 2, 128, KC * 128])
    w_out_d = din("w_out_t", [c.CB * c.KQ, 128, c.K8 * 512])
    w_dn_d = din("w_dn_t", [c.NGRP * c.CB, 128, c.NG * 512])
    pool_d = din("pool_w_t", [4, 128, c.PGC * c.PGW])
    out_d = nc.dram_tensor("out", [c.TPC, D], F32, kind="ExternalOutput").ap()

    def sb(name, shape, dt=F32):
        return nc.alloc_sbuf_tensor(name, list(shape), dt)

    ident_f = sb("ident_f", [128, 128])
    ident_b = sb("ident_b", [128, 128], BF16)
    ones_b = sb("ones_b", [128, 128], BF16)
    ones_f = sb("ones_f", [128, 128])
    eps_t = sb("eps_t", [128, 1])
    gq_t = sb("gq_t", [128, 1])
    gqs_t = sb("gqs_t", [128, 1])
    gk_t = sb("gk_t", [128, 1])
    sink_t = sb("sink_t", [128, H])
    es_t = sb("es_t", [128, H])
    pscale_t = sb("pscale_t", [128, c.PC])
    kmask_t = sb("kmask_t", [128, c.NPASS * c.NE])
    ssq = sb("ssq", [128, 8])
    lnv = sb("lnv", [128, 8])
    rstd = sb("rstd", [128, 8])
    scr = sb("scr", [128, 8])

    base = (nc._sbuf_addr_for_side("left") + 63) // 64 * 64
    LIMIT = 229344

    def at(name, shape, dt, off):
        nbytes = int(np.prod(shape[1:])) * (4 if dt == F32 else 2)
        assert base + off + nbytes <= LIMIT, (name, base, off, nbytes)
        return nc.alloc_sbuf_tensor_at(name, list(shape), dt, offset=base + off)

    KiB = 1024
    X1_SZ = c.NT * D * 4
    RA_SZ = max(KC * TT * 2, KC * T * 2 + H * 128 * 4 + 2 * c.PGC * c.PGW * 2, 6 * KC * 128 * 2)
    RB_SZ = max(2 * D * 4 + D * 4, (H * T + KV * TT + c.NE * KV * 128 + c.PC * T) * 2,
                KC * T * 2 + max(D * 4, 2 * c.NG * T * 2))
    WT_SZ = max(KC * 128 * 2, c.K8 * 512 * 2, c.NG * 512 * 2)
    RC_SZ = max(3 * WT_SZ, 3 * H * 128 * 4)
    oX1, oRA = 0, X1_SZ
    oRB = oRA + RA_SZ
    oRC = oRB + RB_SZ
    oRD = oRC + RC_SZ

    x1 = at("x1", [128, c.NT, D], F32, oX1)
    hT = at("hT", [128, KC, TT], BF16, oRA)
    mixedT = at("mixedT", [128, KC, T], BF16, oRA)
    esink_all = at("esink_all", [128, H, 128], F32, oRA + KC * T * 2)
    pwt = [at(f"pwt{i}", [128, c.PGC, c.PGW], BF16, oRA + KC * T * 2 + H * 128 * 4 + i * c.PGC * c.PGW * 2)
           for i in range(2)]
    guw = [at(f"guw{i}", [128, KC, 128], BF16, oRA + i * KC * 128 * 2) for i in range(6)]
    xh = [at(f"xh{i}", [128, D], F32, oRB + i * D * 4) for i in range(2)]
    g1b = at("g1b_s", [128, D], F32, oRB + 2 * D * 4)
    qT = at("qT", [128, H, T], BF16, oRB)
    kT = at("kT", [128, KV, TT], BF16, oRB + H * T * 2)
    vtok = at("vtok", [128, c.NE, KV * 128], BF16, oRB + H * T * 2 + KV * TT * 2)
    uT = at("uT", [128, c.PC, T], BF16, oRB + H * T * 2 + KV * TT * 2 + c.NE * KV * 128 * 2)
    h2T = at("h2T", [128, KC, T], BF16, oRB)
    g2b = at("g2b_s", [128, D], F32, oRB + KC * T * 2)
    actb = [at(f"act{i}", [128, c.NG, T], BF16, oRB + KC * T * 2 + i * c.NG * T * 2) for i in range(2)]
    wring = [at(f"wr{i}", [128, WT_SZ // 2], BF16, oRC + i * WT_SZ) for i in range(3)]
    bias_all = at("bias_all_s", [128, 3, H * 128], F32, oRC)
    htok = [at(f"htok{i}", [128, D], BF16, oRD + i * D * 2) for i in range(2)]
    o = oRD
    sqb = at("sqb", [128, 512], BF16, o); o += 1024
    lnb = at("lnb", [128, 512], F32, o); o += 2048
    rsb = at("rsb", [128, 512], F32, o); o += 2048
    pbuf = at("pbuf", [128, 528], F32, o); o += 2112
    sA = at("sA", [128, 528], F32, o); o += 2112
    sB = at("sB", [128, 528], F32, o); o += 2112
    vtmp = at("vtmp", [128, 512], BF16, o); o += 1024
    rcb = at("rcb_s", [128, 4, T], F32, o); o += 4 * T * 4
    o = oRD
    lg = []
    for i in range(3):
        lg.append(at(f"lg{i}", [128, 512], F32, o)); o += 2048
    PTb = []
    for i in range(2):
        PTb.append(at(f"PT{i}", [128, 3, 512], BF16, o)); o += 3072
    dnb = at("dnb", [128, 512], F32, o); o += 2048
    lnd = at("lnd", [128, 512], F32, o); o += 2048
    rden = []
    for i in range(2):
        rden.append(at(f"rden{i}", [128, 512], F32, o)); o += 2048
    sg = [at(f"sg{i}", [128, 512], F32, oRD + i * 2048) for i in range(2)]

    ps = nc.alloc_psum_tensor("ps", [128, 8, 512], F32)

    def bank(b, n=512):
        return ps[:, b, 0:n]

    def bank_bf(b):
        return ps[:, b, :].bitcast(BF16).rearrange("p (a b) -> p a b", a=8)

    def mm(out, lhsT, rhs, start, stop):
        return lambda e: e.matmul(out=out, lhsT=lhsT, rhs=rhs, start=start, stop=stop)

    def tr(out, in_):
        return lambda e: e.transpose(out=out, in_=in_, identity=ident_b[:, :])

    def act(out, in_, func, bias=None, scale=None, accum_out=None):
        kw = {}
        if bias is not None:
            kw["bias"] = bias
        if scale is not None:
            kw["scale"] = scale
        if accum_out is not None:
            kw["accum_out"] = accum_out
        return lambda e: e.activation(out=out, in_=in_, func=func, **kw)

    def tt(out, in0, in1, op):
        return lambda e: e.tensor_tensor(out=out, in0=in0, in1=in1, op=op)

    def stt(out, in0, scalar, in1, op0, op1):
        return lambda e: e.scalar_tensor_tensor(out=out, in0=in0, scalar=scalar, in1=in1, op0=op0, op1=op1)

    def ts(out, in0, s1, op0):
        return lambda e: e.tensor_scalar(out=out, in0=in0, scalar1=s1, scalar2=None, op0=op0)

    def cp(out, in_):
        return lambda e: e.tensor_copy(out=out, in_=in_)

    def dma(out, in_):
        return lambda e: e.dma_start(out=out, in_=in_)

    def mset(ap, v):
        return lambda e: e.memset(ap, v)

    def copy_on(eng, out, in_):
        if eng == ACT:
            return act(out, in_, AF.Copy)
        return cp(out, in_)

    def barrier():
        P.nbar += 1
        toks = [
            P.emit(ACT, act(scr[:, 0:1], eps_t[:, 0:1], AF.Copy), sig="bar_act"),
            P.emit(DVE, mset(scr[:, 1:2], 0.0), sig="bar_dve"),
            P.emit(POOL, mset(scr[:, 2:3], 0.0), sig="bar_pool"),
        ]
        for e in ENGS:
            for t_ in toks:
                P.after(e, t_)

    ring_use = {}

    def ring_full_tok(name, slot):
        return (f"{name}{slot}", 16 * ring_use[(name, slot)])

    ost_tok = [None] * c.NT

    def ring_load(name, slot, dst_ap, src_ap, free_tok):
        ring_use[(name, slot)] = ring_use.get((name, slot), 0) + 1
        return P.emit(POOL, dma(dst_ap, src_ap), waits=[free_tok], sig=f"{name}{slot}", amt=16)

    def xsrc(e):
        return xh[0][:, :] if e == 0 else (xh[1][:, :] if e == c.NE - 1 else x1[:, e - 1, :])

    def issue_x_loads(p):
        toks = []
        for e in range(c.NE):
            r0 = p * T + e * 128
            w_ = [ost_tok[e - 1]] if 1 <= e <= c.NT else []
            toks.append(P.emit(SP, dma(xsrc(e), x_ext[r0:r0 + 128, :]), waits=w_, sig=f"xl{e}", amt=16))
            if e == 0:
                t_g = P.emit(SP, dma(g1b[:, :], g1b_d), sig="g1", amt=16)
        return toks, t_g

    def norm_transpose(n_tiles, src_of, ready_of, gb, t_g, dstT, bank_wait):
        t_trl, t_ev, t_h = {}, {}, {}
        evi = [0]

        def stage1(e):
            hb = htok[e % 2]
            xs = src_of(e)
            t_sq = P.emit(ACT, act(hb[:, :], xs, AF.Square, accum_out=ssq[:, e:e + 1]),
                          waits=list(ready_of(e)) + [t_trl.get(e - 2)], sig="a_x")
            t_ln = P.emit(ACT, act(lnv[:, e:e + 1], ssq[:, e:e + 1], AF.Ln, bias=eps_t[:, 0:1], scale=1.0 / D),
                          waits=[t_sq], sig="a_x")
            t_rs = P.emit(ACT, act(rstd[:, e:e + 1], lnv[:, e:e + 1], AF.Exp, scale=-0.5), waits=[t_ln], sig="a_x")
            t_h[e] = P.emit(DVE, stt(hb[:, :], xs, rstd[:, e:e + 1], gb[:, :], ALU.mult, ALU.mult),
                            waits=[t_rs, t_g] + list(ready_of(e)), sig="d_h")

        def stage2(e):
            hb = htok[e % 2]
            for b in range(c.TRB):
                bk = (e % 2) * 4 + b
                nk = min(8, KC - b * 8)
                tk = None
                for j in range(nk):
                    kc = b * 8 + j
                    tk = P.emit(PE, tr(bank_bf(bk)[:, j, :], hb[:, kc * 128:(kc + 1) * 128]),
                                waits=[t_h[e], t_ev.get(bk), bank_wait.get(bk)],
                                sig=("pe_tr" if j == nk - 1 else None))
                eng = ACT if evi[0] % 2 == 0 else DVE
                evi[0] += 1
                t_ev[bk] = P.emit(eng, copy_on(eng, dstT[:, b * 8:b * 8 + nk, e * 128:(e + 1) * 128],
                                               bank_bf(bk)[:, 0:nk, :]),
                                  waits=[tk], sig=f"ev_{eng}")
                t_trl[e] = tk

        stage1(0)
        for e in range(n_tiles):
            if e + 1 < n_tiles:
                stage1(e + 1)
            stage2(e)

    cst = []
    for dst, src_ in ((ident_f, ident_d), (gq_t, gq_d), (gk_t, gk_d), (sink_t, sink_d),
                      (pscale_t, pscale_d), (kmask_t, kmask_d)):
        cst.append(P.emit(SP, dma(dst[:, :], src_), sig="cst", amt=16))
    t_cst = cst[-1]
    next_x = issue_x_loads(0)
    P.emit(DVE, cp(ident_b[:, :], ident_f[:, :]), waits=[t_cst])
    P.emit(DVE, mset(ones_b[:, :], 1.0))
    P.emit(DVE, mset(ones_f[:, :], 1.0))
    P.emit(DVE, mset(eps_t[:, :], RMS_EPS))
    P.emit(DVE, ts(gqs_t[:, :], gq_t[:, :], float(c.HD) ** -0.5, ALU.mult))
    P.emit(ACT, act(es_t[:, :], sink_t[:, :], AF.Exp), waits=[t_cst])
    barrier()

    for p in range(c.NPASS):
        t_xl, t_g1 = next_x
        b_pref = {}
        for m in range(min(3, c.INC)):
            ring_load("wf", m % 3, wring[m % 3][:, 0:KC * 128], w_in_d[m], None)
            b_pref[m] = ring_full_tok("wf", m % 3)
        norm_transpose(c.NE, xsrc, lambda e: [t_xl[e]], g1b, t_g1, hT, {})
        barrier()

        t_rc = P.emit(SP, dma(rcb[:, :, :], rc_d[:, p * 4 * T:(p + 1) * 4 * T].rearrange("q (a b) -> q a b", a=4)),
                      sig="rc", amt=16)
        slot_free = {}
        acc_free = {}
        sec_free = {}
        sq_free = [None]
        rs_free = [None]
        vt_free = [None]
        pb_free = [None]
        deferred = []
        dve_prev = [None]

        def dve_chain(fn, waits=()):
            tok_ = P.emit(DVE, fn, waits=list(waits) + [dve_prev[0]], sig="d_c")
            dve_prev[0] = tok_
            return tok_

        sub = 0
        for m in range(c.INC):
            slot = m % 3
            wt = wring[slot][:, 0:KC * 128].rearrange("q (k j) -> q k j", k=KC)
            if m in b_pref:
                t_full = b_pref[m]
            else:
                ring_load("wf", slot, wring[slot][:, 0:KC * 128], w_in_d[m], slot_free.get(slot))
                t_full = ring_full_tok("wf", slot)
            if m < H:
                kind, subs = "q", [(128, T)]
            elif m < H + KV:
                kind, subs = "k", [(0, 512), (512, TT - 512)]
            elif m < H + 2 * KV:
                kind, subs = "v", [(0, 512), (512, TT - 512)]
            else:
                kind, subs = "p", [(120, 512)]
            for si, (t0, n) in enumerate(subs):
                ab = sub % 3
                sbk = 3 + sub % 3
                sub += 1
                last_sub = si == len(subs) - 1
                tk = None
                for kc in range(KC):
                    tk = P.emit(PE, mm(bank(ab, n), wt[:, kc, :], hT[:, kc, t0:t0 + n], kc == 0, kc == KC - 1),
                                waits=[t_full, acc_free.get(ab)], sig=("pe_b" if kc == KC - 1 else None))
                t_acc = tk
                if kind == "p":
                    for kc in range(KC):
                        tk = P.emit(PE, mm(bank(sbk, 16), wt[:, kc, :], hT[:, kc, 632:648], kc == 0, kc == KC - 1),
                                    waits=[sec_free.get(sbk)], sig=("pe_b" if kc == KC - 1 else None))
                    t_acc2 = tk
                if last_sub:
                    slot_free[slot] = tk
                for fn_ in deferred:
                    fn_()
                deferred = []
                if kind in ("q", "k"):
                    gain = gqs_t if kind == "q" else gk_t
                    if kind == "q":
                        dst = qT[:, m, 0:n]
                    else:
                        dst = kT[:, m - H, t0:t0 + n]
                    t_sq = P.emit(ACT, act(sqb[:, 0:n], bank(ab, n), AF.Square), waits=[t_acc, sq_free[0]], sig="a_b")

                    def later(ab=ab, sbk=sbk, n=n, t_sq=t_sq, gain=gain, dst=dst):
                        t_ss = P.emit(PE, mm(bank(sbk, n), ones_b[:, :], sqb[:, 0:n], True, True),
                                      waits=[t_sq, sec_free.get(sbk)], sig="pe_b2")
                        sq_free[0] = t_ss
                        t_l = P.emit(ACT, act(lnb[:, 0:n], bank(sbk, n), AF.Ln, bias=eps_t[:, 0:1], scale=1.0 / c.HD),
                                     waits=[t_ss], sig="a_b")
                        sec_free[sbk] = t_l
                        t_r = P.emit(ACT, act(rsb[:, 0:n], lnb[:, 0:n], AF.Exp, scale=-0.5),
                                     waits=[t_l, rs_free[0]], sig="a_b")
                        t_q = P.emit(DVE, stt(dst, bank(ab, n), gain[:, 0:1], rsb[:, 0:n], ALU.mult, ALU.mult),
                                     waits=[t_r], sig="d_b")
                        rs_free[0] = t_q
                        acc_free[ab] = t_q
                    deferred.append(later)
                elif kind == "v":
                    kvh = m - H - KV
                    t_vt = P.emit(ACT, act(vtmp[:, 0:n], bank(ab, n), AF.Copy), waits=[t_acc, vt_free[0]], sig="a_b")
                    acc_free[ab] = t_vt

                    def later(sbk=sbk, n=n, t0=t0, t_vt=t_vt, kvh=kvh):
                        nt_ = n // 128
                        tk_ = None
                        for j in range(nt_):
                            tk_ = P.emit(PE, tr(bank_bf(sbk)[:, j, :], vtmp[:, j * 128:(j + 1) * 128]),
                                         waits=[t_vt, sec_free.get(sbk)], sig=("pe_b2" if j == nt_ - 1 else None))
                        vt_free[0] = tk_
                        e0 = t0 // 128
                        t_c = P.emit(DVE, cp(vtok[:, e0:e0 + nt_, kvh * 128:(kvh + 1) * 128], bank_bf(sbk)[:, 0:nt_, :]),
                                     waits=[tk_], sig="d_b")
                        sec_free[sbk] = t_c
                    deferred.append(later)
                else:
                    pc = m - H - 2 * KV
                    g = pc // c.PGC
                    t_p1 = P.emit(ACT, act(pbuf[:, 0:512], bank(ab, 512), AF.Copy), waits=[t_acc, pb_free[0]], sig="a_b")
                    acc_free[ab] = t_p1
                    t_p2 = P.emit(ACT, act(pbuf[:, 512:528], bank(sbk, 16), AF.Copy), waits=[t_acc2], sig="a_b")
                    sec_free[sbk] = t_p2
                    cur, L = pbuf, 528
                    bufs = [sA, sB]
                    bi = 0
                    first = True
                    shift = 1
                    for _ in range(g + 1):
                        nxt = bufs[bi]
                        bi ^= 1
                        L2 = L - shift
                        dve_chain(tt(nxt[:, 0:L2], cur[:, 0:L2], cur[:, shift:shift + L2], ALU.add),
                                  waits=([t_p1, t_p2] if first else []))
                        first = False
                        cur, L = nxt, L2
                        shift *= 2
                    w_ = 2 << g
                    off = 8 - w_ // 2
                    nxt = bufs[bi]
                    dve_chain(tt(nxt[:, 0:T], cur[:, off:off + T], rcb[:, g, :], ALU.mult), waits=[t_rc])
                    t_u = dve_chain(tt(uT[:, pc, :], nxt[:, 0:T], pbuf[:, 8:8 + T], ALU.subtract))
                    pb_free[0] = t_u
        for fn_ in deferred:
            fn_()
        deferred = []
        barrier()

        t_bias = [P.emit(SP, dma(bias_all[:, kb, :], bias_d[:, kb * H * 128:(kb + 1) * H * 128]), sig=f"bias{kb}", amt=16)
                  for kb in range(3)]
        t_es = P.emit(DVE, cp(esink_all[:, :, :], es_t[:, :].unsqueeze(2).to_broadcast([128, H, 128])), sig="d_c2")
        pw_tok = []
        for g in range(4):
            if g < 2:
                pw_tok.append(P.emit(POOL, dma(pwt[g][:, :, :], pool_d[g].rearrange("q (a b) -> q a b", a=c.PGC)),
                                     sig=f"pw{g}", amt=16))
        s_free = {}
        lg_free = {}
        pt_free = {}
        o_free = {}
        d_free = {}
        dn_free = [None]
        rd_free = {}
        items = [(t, kvh) for t in range(c.NT) for kvh in range(KV)]
        si_ = 0
        st_pts, st_o, st_r = {}, {}, {}

        def stage_x(idx):
            nonlocal si_
            t, kvh = items[idx]
            pp = idx % 2
            t_pts = []
            for kb in range(3):
                sbk = si_ % 3
                li = si_ % 3
                si_ += 1
                e = t + kb
                t_s = P.emit(PE, mm(bank(sbk, GW).rearrange("q (a b) -> q a b", a=G), kT[:, kvh, e * 128:(e + 1) * 128],
                                    qT[:, kvh * G:(kvh + 1) * G, t * 128:(t + 1) * 128], True, True),
                             waits=[s_free.get(sbk)], sig="pe_s")
                t_lg = P.emit(DVE, tt(lg[li][:, 0:GW], bank(sbk, GW), bias_all[:, kb, kvh * GW:(kvh + 1) * GW], ALU.add),
                              waits=[t_s, t_bias[kb], lg_free.get(li)], sig="d_lg")
                s_free[sbk] = t_lg
                t_pt = P.emit(ACT, act(PTb[pp][:, kb, 0:GW], lg[li][:, 0:GW], AF.Exp,
                                       bias=kmask_t[:, p * c.NE + e:p * c.NE + e + 1]),
                              waits=[t_lg, pt_free.get(pp)], sig="a_pt")
                lg_free[li] = t_pt
                t_pts.append(t_pt)
            st_pts[idx] = t_pts

        def stage_y(idx):
            t, kvh = items[idx]
            pp = idx % 2
            ob = 3 + idx % 3
            db = 6 + idx % 2
            t_pts = st_pts[idx]
            tk_ = None
            for kb in range(3):
                e = t + kb
                tk_ = P.emit(PE, mm(bank(ob, GW), vtok[:, e, kvh * 128:(kvh + 1) * 128], PTb[pp][:, kb, 0:GW],
                                    kb == 0, kb == 2),
                             waits=[t_pts[kb], o_free.get(ob)])
            for kb in range(3):
                tk_ = P.emit(PE, mm(bank(db, GW), ones_b[:, :], PTb[pp][:, kb, 0:GW], kb == 0, kb == 2),
                             waits=[d_free.get(db)], sig=("pe_o" if kb == 2 else None))
            pt_free[pp] = tk_
            st_o[idx] = tk_
            t_dn = P.emit(DVE, tt(dnb[:, 0:GW].rearrange("q (a b) -> q a b", a=G),
                                  bank(db, GW).rearrange("q (a b) -> q a b", a=G),
                                  esink_all[:, kvh * G:(kvh + 1) * G, :], ALU.add),
                          waits=[tk_, t_es, dn_free[0]], sig="d_dn")
            d_free[db] = t_dn
            t_l = P.emit(ACT, act(lnd[:, 0:GW], dnb[:, 0:GW], AF.Ln), waits=[t_dn], sig="a_dn")
            dn_free[0] = t_l
            st_r[idx] = P.emit(ACT, act(rden[idx % 2][:, 0:GW], lnd[:, 0:GW], AF.Exp, scale=-1.0),
                               waits=[t_l, rd_free.get(idx % 2)], sig="a_dn")

        def stage_z(idx):
            t, kvh = items[idx]
            ob = 3 + idx % 3
            t_mx = P.emit(DVE, tt(mixedT[:, kvh * G:(kvh + 1) * G, t * 128:(t + 1) * 128],
                                  bank(ob, GW).rearrange("q (a b) -> q a b", a=G),
                                  rden[idx % 2][:, 0:GW].rearrange("q (a b) -> q a b", a=G), ALU.mult),
                          waits=[st_r[idx], st_o[idx]], sig="d_mx")
            rd_free[idx % 2] = t_mx
            o_free[ob] = t_mx

        nit = len(items)
        for it in range(nit + 2):
            if it < nit:
                stage_x(it)
            if 0 <= it - 1 < nit:
                stage_y(it - 1)
            if 0 <= it - 2 < nit:
                stage_z(it - 2)
        pw_free = {}
        pm_free = {}
        pmi = 0
        for g in range(4):
            if g >= 2:
                pw_tok.append(P.emit(POOL, dma(pwt[g % 2][:, :, :], pool_d[g].rearrange("q (a b) -> q a b", a=c.PGC)),
                                     waits=[pw_free.get(g % 2)], sig=f"pw{g % 2}", amt=16))
            t_pw = (f"pw{g % 2}", 16 * (p * 2 + g // 2 + 1))
            for oc in range(c.PGC):
                bk = pmi % 3
                pmi += 1
                tk = None
                for kc in range(c.PGC):
                    tk = P.emit(PE, mm(bank(bk, T), pwt[g % 2][:, kc, oc * 128:(oc + 1) * 128], uT[:, g * c.PGC + kc, :],
                                       kc == 0, kc == c.PGC - 1),
                                waits=[t_pw, pm_free.get(bk), s_free.get(bk)],
                                sig=("pe_pm" if kc == c.PGC - 1 else None))
                ch = g * c.PGC + oc
                pm_free[bk] = P.emit(ACT, act(mixedT[:, H + ch, :], bank(bk, T), AF.Copy, scale=pscale_t[:, ch:ch + 1]),
                                     waits=[tk], sig="a_pm")
            pw_free[g % 2] = tk
        barrier()

        t_g2 = P.emit(SP, dma(g2b[:, :], g2b_d), sig="g2", amt=16)
        slot_free = {}
        bank_free = {}
        t_evl = {}
        wi = 0
        for cb in range(c.CB):
            toks_t = [None] * c.NT
            for kq in range(c.KQ):
                slot = wi % 3
                wi += 1
                n_el = c.K8 * 512
                piece_tok = None
                if wi == 1 and c.K8 % 4 == 0:
                    pl = n_el // 4
                    piece_tok = []
                    for q_ in range(4):
                        piece_tok.append(P.emit(POOL, dma(wring[slot][:, q_ * pl:(q_ + 1) * pl],
                                                          w_out_d[cb * c.KQ + kq][:, q_ * pl:(q_ + 1) * pl]),
                                                sig=f"wfp{q_}", amt=16))
                    t_full = None
                else:
                    ring_load("wf", slot, wring[slot][:, 0:n_el], w_out_d[cb * c.KQ + kq], slot_free.get(slot))
                    t_full = ring_full_tok("wf", slot)
                wt = wring[slot][:, 0:n_el].rearrange("q (k j) -> q k j", k=c.K8)
                tk = None
                for k8 in range(c.K8):
                    kc = kq * c.K8 + k8
                    for t in range(c.NT):
                        bk = (cb % 2) * 4 + t
                        lastk = kc == KC - 1
                        tk = P.emit(PE, mm(bank(bk, 512), mixedT[:, kc, t * 128:(t + 1) * 128], wt[:, k8, :],
                                           kc == 0, lastk),
                                    waits=[t_full if piece_tok is None else piece_tok[k8 // (c.K8 // 4)],
                                           bank_free.get(bk)],
                                    sig=("pe_d" if (lastk or (k8 == c.K8 - 1 and t == c.NT - 1)) else None))
                        if lastk:
                            toks_t[t] = tk
                slot_free[slot] = tk
            for t in range(c.NT):
                bk = (cb % 2) * 4 + t
                xs = x1[:, t, cb * 512:(cb + 1) * 512]
                t_evl[t] = P.emit(DVE, tt(xs, bank(bk, 512), xs, ALU.add), waits=[toks_t[t]], sig="d_ev")
                bank_free[bk] = t_evl[t]
        gu_pref = {}
        for ch in range(min(3, c.FFC)):
            for which in range(2):
                i = 2 * ch + which
                ring_load("gu", i % 6, guw[i % 6][:, :, :], w_gu_d[ch, which].rearrange("q (k j) -> q k j", k=KC), tk)
                gu_pref[i] = ring_full_tok("gu", i % 6)
        norm_transpose(c.NT, lambda t: x1[:, t, :], lambda t: [t_evl[t]], g2b, t_g2, h2T, bank_free)
        barrier()

        gu_free = {}
        dn_slot_free = {}
        gb_free = {}
        ub_free = {}
        sg_free = {}
        act_free = {}
        dbank_free = {}
        t_act_last = {}
        t_dlast = {}
        dwi = [0]

        def down_cb(grp, cb):
            n_g = min(c.NG, c.FFC - grp * c.NG)
            ab_ = actb[grp % 2]
            slot = dwi[0] % 3
            dwi[0] += 1
            ring_load("wf", slot, wring[slot][:, 0:n_g * 512], w_dn_d[grp * c.CB + cb][:, 0:n_g * 512],
                      dn_slot_free.get(slot))
            t_full = ring_full_tok("wf", slot)
            wt = wring[slot][:, 0:n_g * 512].rearrange("q (k j) -> q k j", k=n_g)
            tk_ = None
            for t in range(c.NT):
                bk = 4 + t
                for j in range(n_g):
                    tk_ = P.emit(PE, mm(bank(bk, 512), ab_[:, j, t * 128:(t + 1) * 128], wt[:, j, :],
                                        j == 0, j == n_g - 1),
                                 waits=[t_full, t_act_last[grp], dbank_free.get(bk)],
                                 sig=("pe_dn" if j == n_g - 1 else None))
                xs = x1[:, t, cb * 512:(cb + 1) * 512]
                dbank_free[bk] = P.emit(DVE, tt(xs, bank(bk, 512), xs, ALU.add), waits=[tk_], sig="d_ev")
                if grp == c.NGRP - 1:
                    r0 = p * T + t * 128
                    ost_tok[t] = P.emit(SP, dma(out_d[r0:r0 + 128, cb * 512:(cb + 1) * 512], xs),
                                        waits=[dbank_free[bk]], sig=f"ost{t}", amt=16)
            dn_slot_free[slot] = tk_
            t_dlast[grp] = tk_

        pend_down = []
        for ch in range(c.FFC):
            grp, j = ch // c.NG, ch % c.NG
            toks = []
            for which in range(2):
                i = 2 * ch + which
                slot = i % 6
                if i in gu_pref:
                    t_full = gu_pref[i]
                else:
                    ring_load("gu", slot, guw[slot][:, :, :], w_gu_d[ch, which].rearrange("q (k j) -> q k j", k=KC),
                              gu_free.get(slot))
                    t_full = ring_full_tok("gu", slot)
                bk = (ch % 2) * 2 + which
                bfree = gb_free.get(bk) if which == 0 else ub_free.get(bk)
                tk = None
                for kc in range(KC):
                    tk = P.emit(PE, mm(bank(bk, T), guw[slot][:, kc, :], h2T[:, kc, :], kc == 0, kc == KC - 1),
                                waits=[t_full, bfree], sig=("pe_g" if kc == KC - 1 else None))
                gu_free[slot] = tk
                toks.append(tk)
            gbk, ubk = (ch % 2) * 2, (ch % 2) * 2 + 1
            t_sg = P.emit(ACT, act(sg[ch % 2][:, 0:T], bank(gbk, T), AF.Silu), waits=[toks[0], sg_free.get(ch % 2)],
                          sig="a_sg")
            gb_free[gbk] = t_sg
            t_a = P.emit(DVE, tt(actb[grp % 2][:, j, :], sg[ch % 2][:, 0:T], bank(ubk, T), ALU.mult),
                         waits=[t_sg, toks[1], t_dlast.get(grp - 2)], sig="d_act")
            sg_free[ch % 2] = t_a
            ub_free[ubk] = t_a
            t_act_last[grp] = t_a
            last_in_grp = (j == c.NG - 1) or (ch == c.FFC - 1)
            if last_in_grp:
                for (g_, cb_) in pend_down:
                    down_cb(g_, cb_)
                pend_down = [(grp, cb_) for cb_ in range(c.CB)]
            elif pend_down:
                g_, cb_ = pend_down.pop(0)
                down_cb(g_, cb_)
        for (g_, cb_) in pend_down:
            down_cb(g_, cb_)
        t_fin = dbank_free[4 + c.NT - 1]
        barrier()
        if p + 1 < c.NPASS:
            next_x = issue_x_loads(p + 1)
    P.emit(SP, lambda e: e.wait_ge(sem_handles[ost_tok[0][0]], ost_tok[0][1]), waits=ost_tok[1:])

    sem_names = sorted(P.cnt.keys())
    sem_handles = {}
    with contextlib.ExitStack() as stack:
        stack.enter_context(nc.allow_low_precision("bf16 matmul operands by design"))
        for s in sem_names:
            sem_handles[s] = stack.enter_context(nc.semaphore(s))
        block = stack.enter_context(nc.Block())

        def run(eng_name):
            def body(e):
                for ws, fn, sig, amt in P.q[eng_name]:
                    for s, v in ws:
                        e.wait_ge(sem_handles[s], v)
                    inst = fn(e)
                    if sig is not None:
                        inst.then_inc(sem_handles[sig], amt)
            return body

        block.tensor(run(PE))
        block.scalar(run(ACT))
        block.vector(run(DVE))
        block.gpsimd(run(POOL))
        block.sync(run(SP))
    return nc


def host_constants(cfg):
    c = cfg
    H = c.H
    slopes = (2.0 ** (-8.0 * np.arange(1, H + 1, dtype=np.float64) / H)).astype(np.float32)
    j = np.arange(128)[:, None]
    i = np.arange(128)[None, :]
    nd = np.zeros((3, 128, 128), np.float64)
    valid = np.zeros((3, 128, 128), bool)
    for pos in range(3):
        dist = (pos - 1) * 128 + j - i
        nd[pos] = np.abs(dist)
        valid[pos] = np.abs(dist) <= 128
    bias = np.empty((128, 3, H, 128), np.float32)
    for h in range(H):
        b = -(slopes[h].astype(np.float32) * nd.astype(np.float32))
        b = np.where(valid, b, np.float32(NEG)).astype(np.float32)
        bias[:, :, h, :] = b.transpose(1, 0, 2)
    bias = np.ascontiguousarray(bias.reshape(128, 3 * H * 128))
    ident = np.eye(128, dtype=np.float32)
    return bias, ident


def per_core_constants(cfg, core):
    c = cfg
    kmask = np.zeros((128, c.NPASS * c.NE), np.float32)
    for p in range(c.NPASS):
        for e in range(c.NE):
            g0 = core * c.TPC + p * c.T + (e - 1) * 128
            if g0 < 0 or g0 >= c.SEQ:
                kmask[:, p * c.NE + e] = NEG
    rc = np.zeros((c.NPASS, 4, c.T), np.float32)
    for p in range(c.NPASS):
        s = core * c.TPC + p * c.T + np.arange(c.T)
        for gi, w in enumerate((2, 4, 8, 16)):
            left = w // 2
            right = w - 1 - left
            lo = np.clip(s - left, 0, c.SEQ)
            hi = np.clip(s + right + 1, 0, c.SEQ)
            rc[p, gi] = (1.0 / (hi - lo).astype(np.float64)).astype(np.float32)
    rcb = np.ascontiguousarray(np.broadcast_to(rc.reshape(1, -1), (128, c.NPASS * 4 * c.T)))
    return kmask, rcb


def tile_weights(cfg, w_in, w_out, w_gate, w_up, w_down, pool_w):
    c = cfg
    KC = c.KC

    def colchunks(w, nchunk):
        K = w.shape[0]
        kc = K // 128
        return np.ascontiguousarray(w.reshape(kc, 128, nchunk, 128).transpose(2, 1, 0, 3).reshape(nchunk, 128, kc * 128))

    w_in_t = colchunks(w_in, c.INC)
    wg = colchunks(w_gate, c.FFC)
    wu = colchunks(w_up, c.FFC)
    w_gu_t = np.ascontiguousarray(np.stack([wg, wu], axis=1))
    del wg, wu
    w_out_t = np.ascontiguousarray(
        w_out.reshape(c.KQ, c.K8, 128, c.CB, 512).transpose(3, 0, 2, 1, 4).reshape(c.CB * c.KQ, 128, c.K8 * 512))
    wd = np.zeros((c.NGRP * c.NG * 128, c.D), np.float32)
    wd[:c.FF] = w_down
    w_dn_t = np.ascontiguousarray(
        wd.reshape(c.NGRP, c.NG, 128, c.CB, 512).transpose(0, 3, 2, 1, 4).reshape(c.NGRP * c.CB, 128, c.NG * 512))
    del wd
    pool_w_t = np.ascontiguousarray(
        pool_w.reshape(4, c.PGC, 128, c.PGW).transpose(0, 2, 1, 3).reshape(4, 128, c.PGC * c.PGW))
    return w_in_t, w_gu_t, w_out_t, w_dn_t, pool_w_t


def make_in_maps(cfg, x, norm1_g, w_in, q_norm_g, k_norm_g, sink_logits, pool_w, pool_scale,
                 w_out, norm2_g, w_gate, w_up, w_down):
    c = cfg
    f = lambda a: np.ascontiguousarray(np.asarray(a, dtype=np.float32))
    x = f(x).reshape(c.SEQ, c.D)
    xp = np.zeros((c.SEQ + 256, c.D), np.float32)
    xp[128:128 + c.SEQ] = x
    bias, ident = host_constants(c)
    w_in_t, w_gu_t, w_out_t, w_dn_t, pool_w_t = tile_weights(
        c, f(w_in), f(w_out), f(w_gate), f(w_up), f(w_down), f(pool_w))
    shared = {
        "g1b": np.ascontiguousarray(np.broadcast_to(f(norm1_g)[None, :], (128, c.D))),
        "g2b": np.ascontiguousarray(np.broadcast_to(f(norm2_g)[None, :], (128, c.D))),
        "gq": f(q_norm_g).reshape(128, 1),
        "gk": f(k_norm_g).reshape(128, 1),
        "sinkb": np.ascontiguousarray(np.broadcast_to(f(sink_logits)[None, :], (128, c.H))),
        "pscale": np.ascontiguousarray(f(pool_scale).reshape(c.PC, 128).T),
        "bias_all": bias,
        "ident": ident,
        "w_in_t": w_in_t, "w_gu_t": w_gu_t, "w_out_t": w_out_t, "w_dn_t": w_dn_t, "pool_w_t": pool_w_t,
    }
    in_maps = []
    for core in range(c.NCORES):
        kmask, rcb = per_core_constants(c, core)
        m = dict(shared)
        m["x_ext"] = np.ascontiguousarray(xp[core * c.TPC: core * c.TPC + c.TPC + 256])
        m["kmask"] = kmask
        m["rcb"] = rcb
        in_maps.append(m)
    return in_maps


_NC_CACHE = {}


def run_cfg(cfg, inputs, trace=False):
    key = (cfg.D, cfg.FF, cfg.H, cfg.KV)
    if key not in _NC_CACHE:
        _NC_CACHE[key] = build_program(cfg)
    nc = _NC_CACHE[key]
    in_maps = make_in_maps(cfg, **inputs)
    res = run_bass_kernel_spmd(nc, in_maps, core_ids=list(range(cfg.NCORES)), trace=trace)
    out = np.concatenate([r["out"] for r in res.results], axis=0)
    return out.reshape(1, cfg.SEQ, cfg.D).astype(np.float32), res


def kernel(x, norm1_g, w_in, q_norm_g, k_norm_g, sink_logits, pool_w, pool_scale,
           w_out, norm2_g, w_gate, w_up, w_down):
    inputs = dict(x=x, norm1_g=norm1_g, w_in=w_in, q_norm_g=q_norm_g, k_norm_g=k_norm_g,
                  sink_logits=sink_logits, pool_w=pool_w, pool_scale=pool_scale, w_out=w_out,
                  norm2_g=norm2_g, w_gate=w_gate, w_up=w_up, w_down=w_down)
    out, _ = run_cfg(FULL, inputs)
    return out
```

```python
import contextlib
import numpy as np
import concourse.bass as bass
import concourse.mybir as mybir
from concourse.alu_op_type import AluOpType as ALU
from concourse.bass_utils import run_bass_kernel_spmd

F32 = mybir.dt.float32
BF16 = mybir.dt.bfloat16
AF = mybir.ActivationFunctionType

PE, ACT, DVE, POOL, SP = "pe", "act", "dve", "pool", "sp"
ENGS = (PE, ACT, DVE, POOL, SP)
RMS_EPS = 1e-6
NEG = -1.0e30


class Cfg:
    def __init__(self, D=4096, FF=11008, H=16, KV=4, SEQ=8192, NCORES=8, T=512, NG=8):
        self.D, self.FF, self.H, self.KV, self.SEQ, self.NCORES, self.T, self.NG = D, FF, H, KV, SEQ, NCORES, T, NG
        self.HD = 128
        self.G = H // KV
        self.GW = self.G * 128
        assert self.GW <= 512
        self.KC = D // 128
        self.AW = H * 128
        self.PW = D - self.AW
        self.PC = self.PW // 128
        self.PGC = self.PC // 4
        self.PGW = self.PGC * 128
        self.INC = H + 2 * KV + self.PC
        self.FFC = FF // 128
        self.NGRP = (self.FFC + NG - 1) // NG
        self.TPC = SEQ // NCORES
        self.NPASS = self.TPC // T
        self.NT = T // 128
        self.NE = self.NT + 2
        self.TT = self.NE * 128
        self.CB = D // 512
        self.KQ = max(1, self.KC // 8)
        self.K8 = self.KC // self.KQ
        self.TRB = (self.KC + 7) // 8
        assert self.TRB <= 4 and self.NT == 4


FULL = Cfg()


class Prog:
    def __init__(self):
        self.q = {e: [] for e in ENGS}
        self.cnt = {}
        self.waited = {e: {} for e in ENGS}
        self.pending = {e: [] for e in ENGS}
        self.nbar = 0

    def emit(self, eng, fn, waits=(), sig=None, amt=1):
        ws = []
        allw = list(waits) + self.pending[eng]
        self.pending[eng] = []
        for tok in allw:
            if tok is None:
                continue
            s, v = tok
            if v <= 0 or self.waited[eng].get(s, 0) >= v:
                continue
            self.waited[eng][s] = v
            ws.append((s, v))
        tok = None
        if sig is not None:
            self.cnt[sig] = self.cnt.get(sig, 0) + amt
            tok = (sig, self.cnt[sig])
        self.q[eng].append((ws, fn, sig, amt))
        return tok

    def after(self, eng, tok):
        self.pending[eng].append(tok)


def build_program(cfg):
    c = cfg
    D, T, TT, KC, H, KV, G, GW = c.D, c.T, c.TT, c.KC, c.H, c.KV, c.G, c.GW
    nc = bass.Bass("TRN2", target_bir_lowering=False)
    P = Prog()

    def din(name, shape):
        return nc.dram_tensor(name, list(shape), F32, kind="ExternalInput").ap()

    x_ext = din("x_ext", [c.TPC + 256, D])
    g1b_d = din("g1b", [128, D])
    g2b_d = din("g2b", [128, D])
    gq_d = din("gq", [128, 1])
    gk_d = din("gk", [128, 1])
    sink_d = din("sinkb", [128, H])
    pscale_d = din("pscale", [128, c.PC])
    bias_d = din("bias_all", [128, 3 * H * 128])
    kmask_d = din("kmask", [128, c.NPASS * c.NE])
    rc_d = din("rcb", [128, c.NPASS * 4 * T])
    ident_d = din("ident", [128, 128])
    w_in_d = din("w_in_t", [c.INC, 128, KC * 128])
    w_gu_d = din("w_gu_t", [c.FFC, 2, 128, KC * 128])
    w_out_d = din("w_out_t", [c.CB * c.KQ, 128, c.K8 * 512])
    w_dn_d = din("w_dn_t", [c.NGRP * c.CB, 128, c.NG * 512])
    pool_d = din("pool_w_t", [4, 128, c.PGC * c.PGW])
    out_d = nc.dram_tensor("out", [c.TPC, D], F32, kind="ExternalOutput").ap()

    def sb(name, shape, dt=F32):
        return nc.alloc_sbuf_tensor(name, list(shape), dt)

    ident_f = sb("ident_f", [128, 128])
    ident_b = sb("ident_b", [128, 128], BF16)
    ones_b = sb("ones_b", [128, 128], BF16)
    ones_f = sb("ones_f", [128, 128])
    eps_t = sb("eps_t", [128, 1])
    gq_t = sb("gq_t", [128, 1])
    gqs_t = sb("gqs_t", [128, 1])
    gk_t = sb("gk_t", [128, 1])
    sink_t = sb("sink_t", [128, H])
    es_t = sb("es_t", [128, H])
    pscale_t = sb("pscale_t", [128, c.PC])
    kmask_t = sb("kmask_t", [128, c.NPASS * c.NE])
    ssq = sb("ssq", [128, 8])
    lnv = sb("lnv", [128, 8])
    rstd = sb("rstd", [128, 8])
    scr = sb("scr", [128, 8])

    base = (nc._sbuf_addr_for_side("left") + 63) // 64 * 64
    LIMIT = 229344

    def at(name, shape, dt, off):
        nbytes = int(np.prod(shape[1:])) * (4 if dt == F32 else 2)
        assert base + off + nbytes <= LIMIT, (name, base, off, nbytes)
        return nc.alloc_sbuf_tensor_at(name, list(shape), dt, offset=base + off)

    KiB = 1024
    X1_SZ = c.NT * D * 4
    RA_SZ = max(KC * TT * 2, KC * T * 2 + H * 128 * 4 + 2 * c.PGC * c.PGW * 2, 6 * KC * 128 * 2)
    RB_SZ = max(2 * D * 4 + D * 4, (H * T + KV * TT + c.NE * KV * 128 + c.PC * T) * 2,
                KC * T * 2 + max(D * 4, 2 * c.NG * T * 2))
    WT_SZ = max(KC * 128 * 2, c.K8 * 512 * 2, c.NG * 512 * 2)
    RC_SZ = max(3 * WT_SZ, 3 * H * 128 * 4)
    oX1, oRA = 0, X1_SZ
    oRB = oRA + RA_SZ
    oRC = oRB + RB_SZ
    oRD = oRC + RC_SZ

    x1 = at("x1", [128, c.NT, D], F32, oX1)
    hT = at("hT", [128, KC, TT], BF16, oRA)
    mixedT = at("mixedT", [128, KC, T], BF16, oRA)
    esink_all = at("esink_all", [128, H, 128], F32, oRA + KC * T * 2)
    pwt = [at(f"pwt{i}", [128, c.PGC, c.PGW], BF16, oRA + KC * T * 2 + H * 128 * 4 + i * c.PGC * c.PGW * 2)
           for i in range(2)]
    guw = [at(f"guw{i}", [128, KC, 128], BF16, oRA + i * KC * 128 * 2) for i in range(6)]
    xh = [at(f"xh{i}", [128, D], F32, oRB + i * D * 4) for i in range(2)]
    g1b = at("g1b_s", [128, D], F32, oRB + 2 * D * 4)
    qT = at("qT", [128, H, T], BF16, oRB)
    kT = at("kT", [128, KV, TT], BF16, oRB + H * T * 2)
    vtok = at("vtok", [128, c.NE, KV * 128], BF16, oRB + H * T * 2 + KV * TT * 2)
    uT = at("uT", [128, c.PC, T], BF16, oRB + H * T * 2 + KV * TT * 2 + c.NE * KV * 128 * 2)
    h2T = at("h2T", [128, KC, T], BF16, oRB)
    g2b = at("g2b_s", [128, D], F32, oRB + KC * T * 2)
    actb = [at(f"act{i}", [128, c.NG, T], BF16, oRB + KC * T * 2 + i * c.NG * T * 2) for i in range(2)]
    wring = [at(f"wr{i}", [128, WT_SZ // 2], BF16, oRC + i * WT_SZ) for i in range(3)]
    bias_all = at("bias_all_s", [128, 3, H * 128], F32, oRC)
    htok = [at(f"htok{i}", [128, D], BF16, oRD + i * D * 2) for i in range(2)]
    o = oRD
    sqb = at("sqb", [128, 512], BF16, o); o += 1024
    lnb = at("lnb", [128, 512], F32, o); o += 2048
    rsb = at("rsb", [128, 512], F32, o); o += 2048
    pbuf = at("pbuf", [128, 528], F32, o); o += 2112
    sA = at("sA", [128, 528], F32, o); o += 2112
    sB = at("sB", [128, 528], F32, o); o += 2112
    vtmp = at("vtmp", [128, 512], BF16, o); o += 1024
    rcb = at("rcb_s", [128, 4, T], F32, o); o += 4 * T * 4
    o = oRD
    lg = []
    for i in range(3):
        lg.append(at(f"lg{i}", [128, 512], F32, o)); o += 2048
    PTb = []
    for i in range(2):
        PTb.append(at(f"PT{i}", [128, 3, 512], BF16, o)); o += 3072
    dnb = at("dnb", [128, 512], F32, o); o += 2048
    lnd = at("lnd", [128, 512], F32, o); o += 2048
    rden = []
    for i in range(2):
        rden.append(at(f"rden{i}", [128, 512], F32, o)); o += 2048
    sg = [at(f"sg{i}", [128, 512], F32, oRD + i * 2048) for i in range(2)]

    ps = nc.alloc_psum_tensor("ps", [128, 8, 512], F32)

    def bank(b, n=512):
        return ps[:, b, 0:n]

    def bank_bf(b):
        return ps[:, b, :].bitcast(BF16).rearrange("p (a b) -> p a b", a=8)

    def mm(out, lhsT, rhs, start, stop):
        return lambda e: e.matmul(out=out, lhsT=lhsT, rhs=rhs, start=start, stop=stop)

    def tr(out, in_):
        return lambda e: e.transpose(out=out, in_=in_, identity=ident_b[:, :])

    def act(out, in_, func, bias=None, scale=None, accum_out=None):
        kw = {}
        if bias is not None:
            kw["bias"] = bias
        if scale is not None:
            kw["scale"] = scale
        if accum_out is not None:
            kw["accum_out"] = accum_out
        return lambda e: e.activation(out=out, in_=in_, func=func, **kw)

    def tt(out, in0, in1, op):
        return lambda e: e.tensor_tensor(out=out, in0=in0, in1=in1, op=op)

    def stt(out, in0, scalar, in1, op0, op1):
        return lambda e: e.scalar_tensor_tensor(out=out, in0=in0, scalar=scalar, in1=in1, op0=op0, op1=op1)

    def ts(out, in0, s1, op0):
        return lambda e: e.tensor_scalar(out=out, in0=in0, scalar1=s1, scalar2=None, op0=op0)

    def cp(out, in_):
        return lambda e: e.tensor_copy(out=out, in_=in_)

    def dma(out, in_):
        return lambda e: e.dma_start(out=out, in_=in_)

    def mset(ap, v):
        return lambda e: e.memset(ap, v)

    def copy_on(eng, out, in_):
        if eng == ACT:
            return act(out, in_, AF.Copy)
        return cp(out, in_)

    def barrier():
        P.nbar += 1
        toks = [
            P.emit(ACT, act(scr[:, 0:1], eps_t[:, 0:1], AF.Copy), sig="bar_act"),
            P.emit(DVE, mset(scr[:, 1:2], 0.0), sig="bar_dve"),
            P.emit(POOL, mset(scr[:, 2:3], 0.0), sig="bar_pool"),
        ]
        for e in ENGS:
            for t_ in toks:
                P.after(e, t_)

    ring_use = {}

    def ring_full_tok(name, slot):
        return (f"{name}{slot}", 16 * ring_use[(name, slot)])

    ost_tok = [None] * c.NT

    def ring_load(name, slot, dst_ap, src_ap, free_tok):
        ring_use[(name, slot)] = ring_use.get((name, slot), 0) + 1
        return P.emit(POOL, dma(dst_ap, src_ap), waits=[free_tok], sig=f"{name}{slot}", amt=16)

    def xsrc(e):
        return xh[0][:, :] if e == 0 else (xh[1][:, :] if e == c.NE - 1 else x1[:, e - 1, :])

    def issue_x_loads(p):
        toks = [None] * c.NE
        seq = []
        t_g = None
        for e in range(c.NE):
            seq.append(("x", e))
            if e == 0:
                seq.append(("g", 0))
        hist = []
        for kind, e in seq:
            w_ = [hist[-2]] if len(hist) >= 2 else []
            if kind == "x":
                r0 = p * T + e * 128
                if 1 <= e <= c.NT:
                    w_.append(ost_tok[e - 1])
                tk_ = P.emit(SP, dma(xsrc(e), x_ext[r0:r0 + 128, :]), waits=w_, sig=f"xl{e}", amt=16)
                toks[e] = tk_
            else:
                tk_ = P.emit(SP, dma(g1b[:, :], g1b_d), waits=w_, sig="g1", amt=16)
                t_g = tk_
            hist.append(tk_)
        return toks, t_g

    def norm_transpose(n_tiles, src_of, ready_of, gb, t_g, dstT, bank_wait):
        t_trl, t_ev, t_h = {}, {}, {}
        evi = [0]

        def stage1(e):
            hb = htok[e % 2]
            xs = src_of(e)
            t_sq = P.emit(ACT, act(hb[:, :], xs, AF.Square, accum_out=ssq[:, e:e + 1]),
                          waits=list(ready_of(e)) + [t_trl.get(e - 2)], sig="a_x")
            t_ln = P.emit(ACT, act(lnv[:, e:e + 1], ssq[:, e:e + 1], AF.Ln, bias=eps_t[:, 0:1], scale=1.0 / D),
                          waits=[t_sq], sig="a_x")
            t_rs = P.emit(ACT, act(rstd[:, e:e + 1], lnv[:, e:e + 1], AF.Exp, scale=-0.5), waits=[t_ln], sig="a_x")
            t_h[e] = P.emit(DVE, stt(hb[:, :], xs, rstd[:, e:e + 1], gb[:, :], ALU.mult, ALU.mult),
                            waits=[t_rs, t_g] + list(ready_of(e)), sig="d_h")

        def stage2(e):
            hb = htok[e % 2]
            for b in range(c.TRB):
                bk = (e % 2) * 4 + b
                nk = min(8, KC - b * 8)
                tk = None
                for j in range(nk):
                    kc = b * 8 + j
                    tk = P.emit(PE, tr(bank_bf(bk)[:, j, :], hb[:, kc * 128:(kc + 1) * 128]),
                                waits=[t_h[e], t_ev.get(bk), bank_wait.get(bk)],
                                sig=("pe_tr" if j == nk - 1 else None))
                eng = ACT if evi[0] % 2 == 0 else DVE
                evi[0] += 1
                t_ev[bk] = P.emit(eng, copy_on(eng, dstT[:, b * 8:b * 8 + nk, e * 128:(e + 1) * 128],
                                               bank_bf(bk)[:, 0:nk, :]),
                                  waits=[tk], sig=f"ev_{eng}")
                t_trl[e] = tk

        stage1(0)
        for e in range(n_tiles):
            if e + 1 < n_tiles:
                stage1(e + 1)
            stage2(e)

    cst = []
    for dst, src_ in ((ident_f, ident_d), (gq_t, gq_d), (gk_t, gk_d), (sink_t, sink_d),
                      (pscale_t, pscale_d), (kmask_t, kmask_d)):
        cst.append(P.emit(SP, dma(dst[:, :], src_), sig="cst", amt=16))
    t_cst = cst[-1]
    next_x = issue_x_loads(0)
    P.emit(DVE, cp(ident_b[:, :], ident_f[:, :]), waits=[t_cst])
    P.emit(DVE, mset(ones_b[:, :], 1.0))
    P.emit(DVE, mset(ones_f[:, :], 1.0))
    P.emit(DVE, mset(eps_t[:, :], RMS_EPS))
    P.emit(DVE, ts(gqs_t[:, :], gq_t[:, :], float(c.HD) ** -0.5, ALU.mult))
    P.emit(ACT, act(es_t[:, :], sink_t[:, :], AF.Exp), waits=[t_cst])
    barrier()

    for p in range(c.NPASS):
        t_xl, t_g1 = next_x
        b_pref = {}
        for m in range(min(3, c.INC)):
            ring_load("wf", m % 3, wring[m % 3][:, 0:KC * 128], w_in_d[m], t_xl[c.NE - 2])
            b_pref[m] = ring_full_tok("wf", m % 3)
        norm_transpose(c.NE, xsrc, lambda e: [t_xl[e]], g1b, t_g1, hT, {})
        barrier()

        t_rc = P.emit(SP, dma(rcb[:, :, :], rc_d[:, p * 4 * T:(p + 1) * 4 * T].rearrange("q (a b) -> q a b", a=4)),
                      sig="rc", amt=16)
        slot_free = {}
        acc_free = {}
        sec_free = {}
        sq_free = [None]
        rs_free = [None]
        vt_free = [None]
        pb_free = [None]
        deferred = []
        dve_prev = [None]

        def dve_chain(fn, waits=()):
            tok_ = P.emit(DVE, fn, waits=list(waits) + [dve_prev[0]], sig="d_c")
            dve_prev[0] = tok_
            return tok_

        sub = 0
        for m in range(c.INC):
            slot = m % 3
            wt = wring[slot][:, 0:KC * 128].rearrange("q (k j) -> q k j", k=KC)
            if m in b_pref:
                t_full = b_pref[m]
            else:
                ring_load("wf", slot, wring[slot][:, 0:KC * 128], w_in_d[m], slot_free.get(slot))
                t_full = ring_full_tok("wf", slot)
            if m < H:
                kind, subs = "q", [(128, T)]
            elif m < H + KV:
                kind, subs = "k", [(0, 512), (512, TT - 512)]
            elif m < H + 2 * KV:
                kind, subs = "v", [(0, 512), (512, TT - 512)]
            else:
                kind, subs = "p", [(120, 512)]
            for si, (t0, n) in enumerate(subs):
                ab = sub % 3
                sbk = 3 + sub % 3
                sub += 1
                last_sub = si == len(subs) - 1
                tk = None
                for kc in range(KC):
                    tk = P.emit(PE, mm(bank(ab, n), wt[:, kc, :], hT[:, kc, t0:t0 + n], kc == 0, kc == KC - 1),
                                waits=[t_full, acc_free.get(ab)], sig=("pe_b" if kc == KC - 1 else None))
                t_acc = tk
                if kind == "p":
                    for kc in range(KC):
                        tk = P.emit(PE, mm(bank(sbk, 16), wt[:, kc, :], hT[:, kc, 632:648], kc == 0, kc == KC - 1),
                                    waits=[sec_free.get(sbk)], sig=("pe_b" if kc == KC - 1 else None))
                    t_acc2 = tk
                if last_sub:
                    slot_free[slot] = tk
                for fn_ in deferred:
                    fn_()
                deferred = []
                if kind in ("q", "k"):
                    gain = gqs_t if kind == "q" else gk_t
                    if kind == "q":
                        dst = qT[:, m, 0:n]
                    else:
                        dst = kT[:, m - H, t0:t0 + n]
                    t_sq = P.emit(ACT, act(sqb[:, 0:n], bank(ab, n), AF.Square), waits=[t_acc, sq_free[0]], sig="a_b")

                    def later(ab=ab, sbk=sbk, n=n, t_sq=t_sq, gain=gain, dst=dst):
                        t_ss = P.emit(PE, mm(bank(sbk, n), ones_b[:, :], sqb[:, 0:n], True, True),
                                      waits=[t_sq, sec_free.get(sbk)], sig="pe_b2")
                        sq_free[0] = t_ss
                        t_l = P.emit(ACT, act(lnb[:, 0:n], bank(sbk, n), AF.Ln, bias=eps_t[:, 0:1], scale=1.0 / c.HD),
                                     waits=[t_ss], sig="a_b")
                        sec_free[sbk] = t_l
                        t_r = P.emit(ACT, act(rsb[:, 0:n], lnb[:, 0:n], AF.Exp, scale=-0.5),
                                     waits=[t_l, rs_free[0]], sig="a_b")
                        t_q = P.emit(DVE, stt(dst, bank(ab, n), gain[:, 0:1], rsb[:, 0:n], ALU.mult, ALU.mult),
                                     waits=[t_r], sig="d_b")
                        rs_free[0] = t_q
                        acc_free[ab] = t_q
                    deferred.append(later)
                elif kind == "v":
                    kvh = m - H - KV
                    t_vt = P.emit(ACT, act(vtmp[:, 0:n], bank(ab, n), AF.Copy), waits=[t_acc, vt_free[0]], sig="a_b")
                    acc_free[ab] = t_vt

                    def later(sbk=sbk, n=n, t0=t0, t_vt=t_vt, kvh=kvh):
                        nt_ = n // 128
                        tk_ = None
                        for j in range(nt_):
                            tk_ = P.emit(PE, tr(bank_bf(sbk)[:, j, :], vtmp[:, j * 128:(j + 1) * 128]),
                                         waits=[t_vt, sec_free.get(sbk)], sig=("pe_b2" if j == nt_ - 1 else None))
                        vt_free[0] = tk_
                        e0 = t0 // 128
                        t_c = P.emit(DVE, cp(vtok[:, e0:e0 + nt_, kvh * 128:(kvh + 1) * 128], bank_bf(sbk)[:, 0:nt_, :]),
                                     waits=[tk_], sig="d_b")
                        sec_free[sbk] = t_c
                    deferred.append(later)
                else:
                    pc = m - H - 2 * KV
                    g = pc // c.PGC
                    t_p1 = P.emit(ACT, act(pbuf[:, 0:512], bank(ab, 512), AF.Copy), waits=[t_acc, pb_free[0]], sig="a_b")
                    acc_free[ab] = t_p1
                    t_p2 = P.emit(ACT, act(pbuf[:, 512:528], bank(sbk, 16), AF.Copy), waits=[t_acc2], sig="a_b")
                    sec_free[sbk] = t_p2
                    cur, L = pbuf, 528
                    bufs = [sA, sB]
                    bi = 0
                    first = True
                    shift = 1
                    for _ in range(g + 1):
                        nxt = bufs[bi]
                        bi ^= 1
                        L2 = L - shift
                        dve_chain(tt(nxt[:, 0:L2], cur[:, 0:L2], cur[:, shift:shift + L2], ALU.add),
                                  waits=([t_p1, t_p2] if first else []))
                        first = False
                        cur, L = nxt, L2
                        shift *= 2
                    w_ = 2 << g
                    off = 8 - w_ // 2
                    nxt = bufs[bi]
                    dve_chain(tt(nxt[:, 0:T], cur[:, off:off + T], rcb[:, g, :], ALU.mult), waits=[t_rc])
                    t_u = dve_chain(tt(uT[:, pc, :], nxt[:, 0:T], pbuf[:, 8:8 + T], ALU.subtract))
                    pb_free[0] = t_u
        for fn_ in deferred:
            fn_()
        deferred = []
        barrier()

        t_bias = [P.emit(SP, dma(bias_all[:, kb, :], bias_d[:, kb * H * 128:(kb + 1) * H * 128]), sig=f"bias{kb}", amt=16)
                  for kb in range(3)]
        t_es = P.emit(DVE, cp(esink_all[:, :, :], es_t[:, :].unsqueeze(2).to_broadcast([128, H, 128])), sig="d_c2")
        pw_tok = []
        for g in range(4):
            if g < 2:
                pw_tok.append(P.emit(POOL, dma(pwt[g][:, :, :], pool_d[g].rearrange("q (a b) -> q a b", a=c.PGC)),
                                     sig=f"pw{g}", amt=16))
        s_free = {}
        lg_free = {}
        pt_free = {}
        o_free = {}
        d_free = {}
        dn_free = [None]
        rd_free = {}
        items = [(t, kvh) for t in range(c.NT) for kvh in range(KV)]
        si_ = 0
        st_pts, st_o, st_r = {}, {}, {}

        def stage_x(idx):
            nonlocal si_
            t, kvh = items[idx]
            pp = idx % 2
            t_pts = []
            for kb in range(3):
                sbk = si_ % 3
                li = si_ % 3
                si_ += 1
                e = t + kb
                t_s = P.emit(PE, mm(bank(sbk, GW).rearrange("q (a b) -> q a b", a=G), kT[:, kvh, e * 128:(e + 1) * 128],
                                    qT[:, kvh * G:(kvh + 1) * G, t * 128:(t + 1) * 128], True, True),
                             waits=[s_free.get(sbk)], sig="pe_s")
                t_lg = P.emit(DVE, tt(lg[li][:, 0:GW], bank(sbk, GW), bias_all[:, kb, kvh * GW:(kvh + 1) * GW], ALU.add),
                              waits=[t_s, t_bias[kb], lg_free.get(li)], sig="d_lg")
                s_free[sbk] = t_lg
                t_pt = P.emit(ACT, act(PTb[pp][:, kb, 0:GW], lg[li][:, 0:GW], AF.Exp,
                                       bias=kmask_t[:, p * c.NE + e:p * c.NE + e + 1]),
                              waits=[t_lg, pt_free.get(pp)], sig="a_pt")
                lg_free[li] = t_pt
                t_pts.append(t_pt)
            st_pts[idx] = t_pts

        def stage_y(idx):
            t, kvh = items[idx]
            pp = idx % 2
            ob = 3 + idx % 3
            db = 6 + idx % 2
            t_pts = st_pts[idx]
            tk_ = None
            for kb in range(3):
                e = t + kb
                tk_ = P.emit(PE, mm(bank(ob, GW), vtok[:, e, kvh * 128:(kvh + 1) * 128], PTb[pp][:, kb, 0:GW],
                                    kb == 0, kb == 2),
                             waits=[t_pts[kb], o_free.get(ob)])
            for kb in range(3):
                tk_ = P.emit(PE, mm(bank(db, GW), ones_b[:, :], PTb[pp][:, kb, 0:GW], kb == 0, kb == 2),
                             waits=[d_free.get(db)], sig=("pe_o" if kb == 2 else None))
            pt_free[pp] = tk_
            st_o[idx] = tk_
            t_dn = P.emit(DVE, tt(dnb[:, 0:GW].rearrange("q (a b) -> q a b", a=G),
                                  bank(db, GW).rearrange("q (a b) -> q a b", a=G),
                                  esink_all[:, kvh * G:(kvh + 1) * G, :], ALU.add),
                          waits=[tk_, t_es, dn_free[0]], sig="d_dn")
            d_free[db] = t_dn
            t_l = P.emit(ACT, act(lnd[:, 0:GW], dnb[:, 0:GW], AF.Ln), waits=[t_dn], sig="a_dn")
            dn_free[0] = t_l
            st_r[idx] = P.emit(ACT, act(rden[idx % 2][:, 0:GW], lnd[:, 0:GW], AF.Exp, scale=-1.0),
                               waits=[t_l, rd_free.get(idx % 2)], sig="a_dn")

        def stage_z(idx):
            t, kvh = items[idx]
            ob = 3 + idx % 3
            t_mx = P.emit(DVE, tt(mixedT[:, kvh * G:(kvh + 1) * G, t * 128:(t + 1) * 128],
                                  bank(ob, GW).rearrange("q (a b) -> q a b", a=G),
                                  rden[idx % 2][:, 0:GW].rearrange("q (a b) -> q a b", a=G), ALU.mult),
                          waits=[st_r[idx], st_o[idx]], sig="d_mx")
            rd_free[idx % 2] = t_mx
            o_free[ob] = t_mx

        nit = len(items)
        for it in range(nit + 2):
            if it < nit:
                stage_x(it)
            if 0 <= it - 1 < nit:
                stage_y(it - 1)
            if 0 <= it - 2 < nit:
                stage_z(it - 2)
        d_pref = {}
        t_lastlg = s_free[(si_ - 1) % 3]
        n_el = c.K8 * 512
        for wi_ in range(min(3, c.CB * c.KQ)):
            if wi_ == 0 and c.K8 % 4 == 0:
                pl = n_el // 4
                d_pref[wi_] = [P.emit(POOL, dma(wring[0][:, q_ * pl:(q_ + 1) * pl], w_out_d[0][:, q_ * pl:(q_ + 1) * pl]),
                                      waits=[t_lastlg], sig=f"wfp{q_}", amt=16) for q_ in range(4)]
            else:
                ring_load("wf", wi_ % 3, wring[wi_ % 3][:, 0:n_el], w_out_d[wi_], t_lastlg)
                d_pref[wi_] = ring_full_tok("wf", wi_ % 3)
        pw_free = {}
        pm_free = {}
        pmi = 0
        for g in range(4):
            if g >= 2:
                pw_tok.append(P.emit(POOL, dma(pwt[g % 2][:, :, :], pool_d[g].rearrange("q (a b) -> q a b", a=c.PGC)),
                                     waits=[pw_free.get(g % 2)], sig=f"pw{g % 2}", amt=16))
            t_pw = (f"pw{g % 2}", 16 * (p * 2 + g // 2 + 1))
            for oc in range(c.PGC):
                bk = pmi % 3
                pmi += 1
                tk = None
                for kc in range(c.PGC):
                    tk = P.emit(PE, mm(bank(bk, T), pwt[g % 2][:, kc, oc * 128:(oc + 1) * 128], uT[:, g * c.PGC + kc, :],
                                       kc == 0, kc == c.PGC - 1),
                                waits=[t_pw, pm_free.get(bk), s_free.get(bk)],
                                sig=("pe_pm" if kc == c.PGC - 1 else None))
                ch = g * c.PGC + oc
                pm_free[bk] = P.emit(ACT, act(mixedT[:, H + ch, :], bank(bk, T), AF.Copy, scale=pscale_t[:, ch:ch + 1]),
                                     waits=[tk], sig="a_pm")
            pw_free[g % 2] = tk
        barrier()

        t_g2 = P.emit(SP, dma(g2b[:, :], g2b_d), sig="g2", amt=16)
        slot_free = {}
        bank_free = {}
        t_evl = {}
        wi = 0
        for cb in range(c.CB):
            toks_t = [None] * c.NT
            for kq in range(c.KQ):
                slot = wi % 3
                wi += 1
                n_el = c.K8 * 512
                piece_tok = None
                if (wi - 1) in d_pref:
                    if isinstance(d_pref[wi - 1], list):
                        piece_tok, t_full = d_pref[wi - 1], None
                    else:
                        t_full = d_pref[wi - 1]
                else:
                    ring_load("wf", slot, wring[slot][:, 0:n_el], w_out_d[cb * c.KQ + kq], slot_free.get(slot))
                    t_full = ring_full_tok("wf", slot)
                wt = wring[slot][:, 0:n_el].rearrange("q (k j) -> q k j", k=c.K8)
                tk = None
                for k8 in range(c.K8):
                    kc = kq * c.K8 + k8
                    for t in range(c.NT):
                        bk = (cb % 2) * 4 + t
                        lastk = kc == KC - 1
                        tk = P.emit(PE, mm(bank(bk, 512), mixedT[:, kc, t * 128:(t + 1) * 128], wt[:, k8, :],
                                           kc == 0, lastk),
                                    waits=[t_full if piece_tok is None else piece_tok[k8 // (c.K8 // 4)],
                                           bank_free.get(bk)],
                                    sig=("pe_d" if (lastk or (k8 == c.K8 - 1 and t == c.NT - 1)) else None))
                        if lastk:
                            toks_t[t] = tk
                slot_free[slot] = tk
            for t in range(c.NT):
                bk = (cb % 2) * 4 + t
                xs = x1[:, t, cb * 512:(cb + 1) * 512]
                t_evl[t] = P.emit(DVE, tt(xs, bank(bk, 512), xs, ALU.add), waits=[toks_t[t]], sig="d_ev")
                bank_free[bk] = t_evl[t]
        gu_pref = {}
        for ch in range(min(3, c.FFC)):
            for which in range(2):
                i = 2 * ch + which
                ring_load("gu", i % 6, guw[i % 6][:, :, :], w_gu_d[ch, which].rearrange("q (k j) -> q k j", k=KC), tk)
                gu_pref[i] = ring_full_tok("gu", i % 6)
        norm_transpose(c.NT, lambda t: x1[:, t, :], lambda t: [t_evl[t]], g2b, t_g2, h2T, bank_free)
        barrier()

        gu_free = {}
        dn_slot_free = {}
        gb_free = {}
        ub_free = {}
        sg_free = {}
        act_free = {}
        dbank_free = {}
        t_act_last = {}
        t_dlast = {}
        dwi = [0]

        def down_cb(grp, cb):
            n_g = min(c.NG, c.FFC - grp * c.NG)
            ab_ = actb[grp % 2]
            slot = dwi[0] % 3
            dwi[0] += 1
            ring_load("wf", slot, wring[slot][:, 0:n_g * 512], w_dn_d[grp * c.CB + cb][:, 0:n_g * 512],
                      dn_slot_free.get(slot))
            t_full = ring_full_tok("wf", slot)
            wt = wring[slot][:, 0:n_g * 512].rearrange("q (k j) -> q k j", k=n_g)
            tk_ = None
            for t in range(c.NT):
                bk = 4 + t
                for j in range(n_g):
                    tk_ = P.emit(PE, mm(bank(bk, 512), ab_[:, j, t * 128:(t + 1) * 128], wt[:, j, :],
                                        j == 0, j == n_g - 1),
                                 waits=[t_full, t_act_last[grp], dbank_free.get(bk)],
                                 sig=("pe_dn" if j == n_g - 1 else None))
                xs = x1[:, t, cb * 512:(cb + 1) * 512]
                dbank_free[bk] = P.emit(DVE, tt(xs, bank(bk, 512), xs, ALU.add), waits=[tk_], sig="d_ev")
                if grp == c.NGRP - 1:
                    r0 = p * T + t * 128
                    ost_tok[t] = P.emit(SP, dma(out_d[r0:r0 + 128, cb * 512:(cb + 1) * 512], xs),
                                        waits=[dbank_free[bk]], sig=f"ost{t}", amt=16)
            dn_slot_free[slot] = tk_
            t_dlast[grp] = tk_

        pend_down = []
        for ch in range(c.FFC):
            grp, j = ch // c.NG, ch % c.NG
            toks = []
            for which in range(2):
                i = 2 * ch + which
                slot = i % 6
                if i in gu_pref:
                    t_full = gu_pref[i]
                else:
                    ring_load("gu", slot, guw[slot][:, :, :], w_gu_d[ch, which].rearrange("q (k j) -> q k j", k=KC),
                              gu_free.get(slot))
                    t_full = ring_full_tok("gu", slot)
                bk = (ch % 2) * 2 + which
                bfree = gb_free.get(bk) if which == 0 else ub_free.get(bk)
                tk = None
                for kc in range(KC):
                    tk = P.emit(PE, mm(bank(bk, T), guw[slot][:, kc, :], h2T[:, kc, :], kc == 0, kc == KC - 1),
                                waits=[t_full, bfree], sig=("pe_g" if kc == KC - 1 else None))
                gu_free[slot] = tk
                toks.append(tk)
            gbk, ubk = (ch % 2) * 2, (ch % 2) * 2 + 1
            t_sg = P.emit(ACT, act(sg[ch % 2][:, 0:T], bank(gbk, T), AF.Silu), waits=[toks[0], sg_free.get(ch % 2)],
                          sig="a_sg")
            gb_free[gbk] = t_sg
            t_a = P.emit(DVE, tt(actb[grp % 2][:, j, :], sg[ch % 2][:, 0:T], bank(ubk, T), ALU.mult),
                         waits=[t_sg, toks[1], t_dlast.get(grp - 2)], sig="d_act")
            sg_free[ch % 2] = t_a
            ub_free[ubk] = t_a
            t_act_last[grp] = t_a
            last_in_grp = (j == c.NG - 1) or (ch == c.FFC - 1)
            if last_in_grp:
                for (g_, cb_) in pend_down:
                    down_cb(g_, cb_)
                pend_down = [(grp, cb_) for cb_ in range(c.CB)]
            elif pend_down:
                g_, cb_ = pend_down.pop(0)
                down_cb(g_, cb_)
        for (g_, cb_) in pend_down:
            down_cb(g_, cb_)
        t_fin = dbank_free[4 + c.NT - 1]
        barrier()
        if p + 1 < c.NPASS:
            next_x = issue_x_loads(p + 1)
    P.emit(SP, lambda e: e.wait_ge(sem_handles[ost_tok[0][0]], ost_tok[0][1]), waits=ost_tok[1:])

    sem_names = sorted(P.cnt.keys())
    sem_handles = {}
    with contextlib.ExitStack() as stack:
        stack.enter_context(nc.allow_low_precision("bf16 matmul operands by design"))
        for s in sem_names:
            sem_handles[s] = stack.enter_context(nc.semaphore(s))
        block = stack.enter_context(nc.Block())

        def run(eng_name):
            def body(e):
                for ws, fn, sig, amt in P.q[eng_name]:
                    for s, v in ws:
                        e.wait_ge(sem_handles[s], v)
                    inst = fn(e)
                    if sig is not None:
                        inst.then_inc(sem_handles[sig], amt)
            return body

        block.tensor(run(PE))
        block.scalar(run(ACT))
        block.vector(run(DVE))
        block.gpsimd(run(POOL))
        block.sync(run(SP))
    return nc


def host_constants(cfg):
    c = cfg
    H = c.H
    slopes = (2.0 ** (-8.0 * np.arange(1, H + 1, dtype=np.float64) / H)).astype(np.float32)
    j = np.arange(128)[:, None]
    i = np.arange(128)[None, :]
    nd = np.zeros((3, 128, 128), np.float64)
    valid = np.zeros((3, 128, 128), bool)
    for pos in range(3):
        dist = (pos - 1) * 128 + j - i
        nd[pos] = np.abs(dist)
        valid[pos] = np.abs(dist) <= 128
    bias = np.empty((128, 3, H, 128), np.float32)
    for h in range(H):
        b = -(slopes[h].astype(np.float32) * nd.astype(np.float32))
        b = np.where(valid, b, np.float32(NEG)).astype(np.float32)
        bias[:, :, h, :] = b.transpose(1, 0, 2)
    bias = np.ascontiguousarray(bias.reshape(128, 3 * H * 128))
    ident = np.eye(128, dtype=np.float32)
    return bias, ident


def per_core_constants(cfg, core):
    c = cfg
    kmask = np.zeros((128, c.NPASS * c.NE), np.float32)
    for p in range(c.NPASS):
        for e in range(c.NE):
            g0 = core * c.TPC + p * c.T + (e - 1) * 128
            if g0 < 0 or g0 >= c.SEQ:
                kmask[:, p * c.NE + e] = NEG
    rc = np.zeros((c.NPASS, 4, c.T), np.float32)
    for p in range(c.NPASS):
        s = core * c.TPC + p * c.T + np.arange(c.T)
        for gi, w in enumerate((2, 4, 8, 16)):
            left = w // 2
            right = w - 1 - left
            lo = np.clip(s - left, 0, c.SEQ)
            hi = np.clip(s + right + 1, 0, c.SEQ)
            rc[p, gi] = (1.0 / (hi - lo).astype(np.float64)).astype(np.float32)
    rcb = np.ascontiguousarray(np.broadcast_to(rc.reshape(1, -1), (128, c.NPASS * 4 * c.T)))
    return kmask, rcb


def tile_weights(cfg, w_in, w_out, w_gate, w_up, w_down, pool_w):
    c = cfg
    KC = c.KC

    def colchunks(w, nchunk):
        K = w.shape[0]
        kc = K // 128
        return np.ascontiguousarray(w.reshape(kc, 128, nchunk, 128).transpose(2, 1, 0, 3).reshape(nchunk, 128, kc * 128))

    w_in_t = colchunks(w_in, c.INC)
    wg = colchunks(w_gate, c.FFC)
    wu = colchunks(w_up, c.FFC)
    w_gu_t = np.ascontiguousarray(np.stack([wg, wu], axis=1))
    del wg, wu
    w_out_t = np.ascontiguousarray(
        w_out.reshape(c.KQ, c.K8, 128, c.CB, 512).transpose(3, 0, 2, 1, 4).reshape(c.CB * c.KQ, 128, c.K8 * 512))
    wd = np.zeros((c.NGRP * c.NG * 128, c.D), np.float32)
    wd[:c.FF] = w_down
    w_dn_t = np.ascontiguousarray(
        wd.reshape(c.NGRP, c.NG, 128, c.CB, 512).transpose(0, 3, 2, 1, 4).reshape(c.NGRP * c.CB, 128, c.NG * 512))
    del wd
    pool_w_t = np.ascontiguousarray(
        pool_w.reshape(4, c.PGC, 128, c.PGW).transpose(0, 2, 1, 3).reshape(4, 128, c.PGC * c.PGW))
    return w_in_t, w_gu_t, w_out_t, w_dn_t, pool_w_t


def make_in_maps(cfg, x, norm1_g, w_in, q_norm_g, k_norm_g, sink_logits, pool_w, pool_scale,
                 w_out, norm2_g, w_gate, w_up, w_down):
    c = cfg
    f = lambda a: np.ascontiguousarray(np.asarray(a, dtype=np.float32))
    x = f(x).reshape(c.SEQ, c.D)
    xp = np.zeros((c.SEQ + 256, c.D), np.float32)
    xp[128:128 + c.SEQ] = x
    bias, ident = host_constants(c)
    w_in_t, w_gu_t, w_out_t, w_dn_t, pool_w_t = tile_weights(
        c, f(w_in), f(w_out), f(w_gate), f(w_up), f(w_down), f(pool_w))
    shared = {
        "g1b": np.ascontiguousarray(np.broadcast_to(f(norm1_g)[None, :], (128, c.D))),
        "g2b": np.ascontiguousarray(np.broadcast_to(f(norm2_g)[None, :], (128, c.D))),
        "gq": f(q_norm_g).reshape(128, 1),
        "gk": f(k_norm_g).reshape(128, 1),
        "sinkb": np.ascontiguousarray(np.broadcast_to(f(sink_logits)[None, :], (128, c.H))),
        "pscale": np.ascontiguousarray(f(pool_scale).reshape(c.PC, 128).T),
        "bias_all": bias,
        "ident": ident,
        "w_in_t": w_in_t, "w_gu_t": w_gu_t, "w_out_t": w_out_t, "w_dn_t": w_dn_t, "pool_w_t": pool_w_t,
    }
    in_maps = []
    for core in range(c.NCORES):
        kmask, rcb = per_core_constants(c, core)
        m = dict(shared)
        m["x_ext"] = np.ascontiguousarray(xp[core * c.TPC: core * c.TPC + c.TPC + 256])
        m["kmask"] = kmask
        m["rcb"] = rcb
        in_maps.append(m)
    return in_maps


_NC_CACHE = {}


def run_cfg(cfg, inputs, trace=False):
    key = (cfg.D, cfg.FF, cfg.H, cfg.KV)
    if key not in _NC_CACHE:
        _NC_CACHE[key] = build_program(cfg)
    nc = _NC_CACHE[key]
    in_maps = make_in_maps(cfg, **inputs)
    res = run_bass_kernel_spmd(nc, in_maps, core_ids=list(range(cfg.NCORES)), trace=trace)
    out = np.concatenate([r["out"] for r in res.results], axis=0)
    return out.reshape(1, cfg.SEQ, cfg.D).astype(np.float32), res


def kernel(x, norm1_g, w_in, q_norm_g, k_norm_g, sink_logits, pool_w, pool_scale,
           w_out, norm2_g, w_gate, w_up, w_down):
    inputs = dict(x=x, norm1_g=norm1_g, w_in=w_in, q_norm_g=q_norm_g, k_norm_g=k_norm_g,
                  sink_logits=sink_logits, pool_w=pool_w, pool_scale=pool_scale, w_out=w_out,
                  norm2_g=norm2_g, w_gate=w_gate, w_up=w_up, w_down=w_down)
    out, _ = run_cfg(FULL, inputs)
    return out
```
